# Optimizing a Trainium2 kernel written in Bass

```python
import functools
import jax, jax.numpy as jnp
from jax import lax
import numpy as np

D_MODEL = 2048
BATCH = 4
SEQ = 8192
DEPTH = 4
DEC_BATCH = 8
DEC_SEQ = 16
PAST_LEN = 1024

CHUNK = 64
N_MIXERS = 2
A_HEADS = 16
A_HEAD_DIM = D_MODEL // A_HEADS
A_PREV_CHUNKS = 8
A_REL_CLIP = 128
B_WINDOW = 128
B_PREV_CHUNKS = B_WINDOW // CHUNK
B_HEAD_DIM = 64
B_Q_HEADS = D_MODEL // B_HEAD_DIM
B_KV_HEADS = 8
B_GROUP = B_Q_HEADS // B_KV_HEADS
D_FF = 4 * D_MODEL
RMS_EPS = 1e-6
NEG_INF = -1e30

kernel_name = 'chunk_streaming_hybrid_encoder_step'


def _n_layers_of(kind):
    return len([i for i in range(DEPTH) if i % N_MIXERS == kind])


def _rmsnorm(x, g):
    x32 = x.astype(jnp.float32)
    y = x32 * lax.rsqrt(jnp.mean(x32 * x32, axis=-1, keepdims=True) + RMS_EPS)
    return (y * g.astype(jnp.float32)).astype(x.dtype)


def _sq_relu_mlp(h, w_up, w_down):
    return jnp.square(jax.nn.relu(h @ w_up)) @ w_down


def _relpos_bias(table, rel):
    idx = jnp.clip(rel, -A_REL_CLIP, A_REL_CLIP) + A_REL_CLIP
    return table.astype(jnp.float32)[:, idx][:, None]


def _alibi_bias(rel):
    slopes = jnp.asarray(2.0 ** (-8.0 * np.arange(1, B_Q_HEADS + 1) / B_Q_HEADS), dtype=jnp.float32)
    b = -slopes[:, None, None] * jnp.abs(rel).astype(jnp.float32)[None]
    return b.reshape((B_KV_HEADS, B_GROUP) + rel.shape)


def _attend(q, k, v, bias, valid, sinks):
    scale = q.shape[-1] ** -0.5
    s = jnp.einsum('bqhgd,bkhd->bhgqk', q, k).astype(jnp.float32) * scale + bias
    s = jnp.where(valid, s, NEG_INF)
    if sinks is not None:
        sink_col = jnp.broadcast_to(sinks.astype(jnp.float32)[None, :, :, None, None], s.shape[:-1] + (1,))
        p = jax.nn.softmax(jnp.concatenate([s, sink_col], axis=-1), axis=-1)[..., :-1]
    else:
        p = jax.nn.softmax(s, axis=-1)
    o = jnp.einsum('bhgqk,bkhd->bqhgd', p.astype(v.dtype), v)
    return o.reshape(o.shape[0], o.shape[1], -1)


def _band_prompt(q, k, v, n_prev, bias_fn, sinks):
    b, s = q.shape[:2]
    pad = n_prev * CHUNK
    band = pad + CHUNK
    kp = jnp.pad(k, ((0, 0), (pad, 0), (0, 0), (0, 0)))
    vp = jnp.pad(v, ((0, 0), (pad, 0), (0, 0), (0, 0)))
    i = jnp.arange(CHUNK)[:, None]
    j = jnp.arange(band)[None, :]
    bias = bias_fn(j - pad - i)

    def one_chunk(c):
        start = c * CHUNK
        qc = lax.dynamic_slice_in_dim(q, start, CHUNK, axis=1)
        kc = lax.dynamic_slice_in_dim(kp, start, band, axis=1)
        vc = lax.dynamic_slice_in_dim(vp, start, band, axis=1)
        valid = j >= pad - start
        return _attend(qc, kc, vc, bias, valid, sinks)

    out = lax.map(one_chunk, jnp.arange(s // CHUNK))
    return out.transpose(1, 0, 2, 3).reshape(b, s, -1)


def _band_sample(q, k_new, v_new, k_cache, v_cache, bias_fn, sinks):
    cl = k_cache.shape[1]
    t = q.shape[1]
    k = jnp.concatenate([k_cache, k_new.astype(k_cache.dtype)], axis=1)
    v = jnp.concatenate([v_cache, v_new.astype(v_cache.dtype)], axis=1)
    qpos = PAST_LEN + jnp.arange(t)
    kpos = jnp.concatenate([PAST_LEN - cl + jnp.arange(cl), PAST_LEN + jnp.arange(t)])
    rel = kpos[None, :] - qpos[:, None]
    valid = jnp.ones(rel.shape, dtype=bool)
    out = _attend(q, k, v, bias_fn(rel), valid, sinks)
    return out, k[:, t:], v[:, t:]


def _qkv_a(h, w_qkv):
    b, s = h.shape[:2]
    q, k, v = jnp.split(h @ w_qkv, 3, axis=-1)
    return (q.reshape(b, s, A_HEADS, 1, A_HEAD_DIM),
            k.reshape(b, s, A_HEADS, A_HEAD_DIM),
            v.reshape(b, s, A_HEADS, A_HEAD_DIM))


def _qkv_b(h, w_qkv):
    b, s = h.shape[:2]
    qkv = h @ w_qkv
    qd = B_Q_HEADS * B_HEAD_DIM
    kd = B_KV_HEADS * B_HEAD_DIM
    return (qkv[..., :qd].reshape(b, s, B_KV_HEADS, B_GROUP, B_HEAD_DIM),
            qkv[..., qd:qd + kd].reshape(b, s, B_KV_HEADS, B_HEAD_DIM),
            qkv[..., qd + kd:].reshape(b, s, B_KV_HEADS, B_HEAD_DIM))


def setup_inputs(seed: int = 0) -> dict:
    key = jax.random.key(seed)
    ks = jax.random.split(key, 17)
    n_a = _n_layers_of(0)
    n_b = _n_layers_of(1)
    cl_a = min(A_PREV_CHUNKS * CHUNK, PAST_LEN)
    cl_b = min(B_PREV_CHUNKS * CHUNK, PAST_LEN)
    f32 = jnp.float32

    def nrm(k, shape, scale):
        return scale * jax.random.normal(k, shape, f32)

    qkv_a_w = 3 * A_HEADS * A_HEAD_DIM
    qkv_b_w = (B_Q_HEADS + 2 * B_KV_HEADS) * B_HEAD_DIM
    return {
        'x_prompt': nrm(ks[0], (BATCH, SEQ, D_MODEL), 1.0),
        'x_sample': nrm(ks[1], (DEC_BATCH, DEC_SEQ, D_MODEL), 1.0),
        'cache_a_k': nrm(ks[2], (n_a, DEC_BATCH, cl_a, A_HEADS, A_HEAD_DIM), 1.0),
        'cache_a_v': nrm(ks[3], (n_a, DEC_BATCH, cl_a, A_HEADS, A_HEAD_DIM), 1.0),
        'cache_b_k': nrm(ks[4], (n_b, DEC_BATCH, cl_b, B_KV_HEADS, B_HEAD_DIM), 1.0),
        'cache_b_v': nrm(ks[5], (n_b, DEC_BATCH, cl_b, B_KV_HEADS, B_HEAD_DIM), 1.0),
        'norm_mix': 1.0 + nrm(ks[6], (DEPTH, D_MODEL), 0.02),
        'norm_ffn': 1.0 + nrm(ks[7], (DEPTH, D_MODEL), 0.02),
        'norm_final': 1.0 + nrm(ks[8], (D_MODEL,), 0.02),
        'a_w_qkv': nrm(ks[9], (n_a, D_MODEL, qkv_a_w), D_MODEL ** -0.5),
        'a_w_o': nrm(ks[10], (n_a, A_HEADS * A_HEAD_DIM, D_MODEL), (A_HEADS * A_HEAD_DIM) ** -0.5),
        'a_rel_bias': nrm(ks[11], (n_a, A_HEADS, 2 * A_REL_CLIP + 1), 0.5),
        'b_w_qkv': nrm(ks[12], (n_b, D_MODEL, qkv_b_w), D_MODEL ** -0.5),
        'b_w_o': nrm(ks[13], (n_b, B_Q_HEADS * B_HEAD_DIM, D_MODEL), (B_Q_HEADS * B_HEAD_DIM) ** -0.5),
        'b_sinks': nrm(ks[14], (n_b, B_Q_HEADS), 1.0),
        'w_up': nrm(ks[15], (DEPTH, D_MODEL, D_FF), D_MODEL ** -0.5),
        'w_down': nrm(ks[16], (DEPTH, D_FF, D_MODEL), D_FF ** -0.5),
    }


def reference(x_prompt, x_sample, cache_a_k, cache_a_v, cache_b_k, cache_b_v,
              norm_mix, norm_ffn, norm_final, a_w_qkv, a_w_o, a_rel_bias,
              b_w_qkv, b_w_o, b_sinks, w_up, w_down):
    xp, xs = x_prompt, x_sample
    s_len = xp.shape[1]
    a_kp, a_vp, a_ks, a_vs = [], [], [], []
    b_kp, b_vp, b_ks, b_vs = [], [], [], []
    for layer in range(DEPTH):
        slot = layer // N_MIXERS
        hp = _rmsnorm(xp, norm_mix[layer])
        hs = _rmsnorm(xs, norm_mix[layer])
        if layer % N_MIXERS == 0:
            bias_fn = functools.partial(_relpos_bias, a_rel_bias[slot])
            qp, kp, vp = _qkv_a(hp, a_w_qkv[slot])
            qs, kn, vn = _qkv_a(hs, a_w_qkv[slot])
            mp = _band_prompt(qp, kp, vp, A_PREV_CHUNKS, bias_fn, None)
            ms, nk, nv = _band_sample(qs, kn, vn, cache_a_k[slot], cache_a_v[slot], bias_fn, None)
            cl = min(A_PREV_CHUNKS * CHUNK, s_len)
            a_kp.append(kp[:, s_len - cl:])
            a_vp.append(vp[:, s_len - cl:])
            a_ks.append(nk)
            a_vs.append(nv)
            xp = xp + mp @ a_w_o[slot]
            xs = xs + ms @ a_w_o[slot]
        else:
            sinks = b_sinks[slot].reshape(B_KV_HEADS, B_GROUP)
            qp, kp, vp = _qkv_b(hp, b_w_qkv[slot])
            qs, kn, vn = _qkv_b(hs, b_w_qkv[slot])
            mp = _band_prompt(qp, kp, vp, B_PREV_CHUNKS, _alibi_bias, sinks)
            ms, nk, nv = _band_sample(qs, kn, vn, cache_b_k[slot], cache_b_v[slot], _alibi_bias, sinks)
            cl = min(B_PREV_CHUNKS * CHUNK, s_len)
            b_kp.append(kp[:, s_len - cl:])
            b_vp.append(vp[:, s_len - cl:])
            b_ks.append(nk)
            b_vs.append(nv)
            xp = xp + mp @ b_w_o[slot]
            xs = xs + ms @ b_w_o[slot]
        xp = xp + _sq_relu_mlp(_rmsnorm(xp, norm_ffn[layer]), w_up[layer], w_down[layer])
        xs = xs + _sq_relu_mlp(_rmsnorm(xs, norm_ffn[layer]), w_up[layer], w_down[layer])
    y_prompt = _rmsnorm(xp, norm_final)
    y_sample = _rmsnorm(xs, norm_final)
    return (y_prompt, y_sample,
            jnp.stack(a_kp), jnp.stack(a_vp), jnp.stack(b_kp), jnp.stack(b_vp),
            jnp.stack(a_ks), jnp.stack(a_vs), jnp.stack(b_ks), jnp.stack(b_vs))
```

```python
import contextlib
import numpy as np
import concourse.bass as bass
import concourse.mybir as mybir
from concourse.bass_utils import run_bass_kernel_spmd

F32 = mybir.dt.float32
BF16 = mybir.dt.bfloat16
AF = mybir.ActivationFunctionType
ALU = mybir.AluOpType

D = 2048
KC = 16
DFF = 8192
CHUNK = 64
NS = 16
EPS = 1e-6
A_H, A_DH, A_CL = 16, 128, 512
B_HQ, B_HKV, B_G, B_DH, B_CL = 32, 8, 4, 64, 128

CFG = dict(DEPTH=4, NTP=4736, SEQ=8192, SPLIT=True, T=512)


class Sem:
    def __init__(self, h):
        self.h = h
        self.count = 0


class Eng:
    def __init__(self, name, sem):
        self.name = name
        self.sem = sem
        self.waited = {}
        self.ops = []


class Buf:
    __slots__ = ("name", "w", "rs", "dsem")

    def __init__(self, name):
        self.name = name
        self.w = None
        self.rs = {}
        self.dsem = None


class _Stop(Exception):
    pass


class Prog:
    def __init__(self, nc, es):
        self.nph = 0
        self.max_ph = None
        self.hook = None
        self.hook_every = 6
        self._npool = 0
        self.nc = nc
        self.es = es
        self.engs = {}
        for nm in ("pe", "act", "dve", "pool", "sp"):
            s = Sem(es.enter_context(nc.semaphore("e_" + nm)))
            self.engs[nm] = Eng(nm, s)
        self.all_dsems = {"sp": [Sem(es.enter_context(nc.semaphore("dh%d" % i))) for i in range(28)],
                          "pool": [Sem(es.enter_context(nc.semaphore("ds%d" % i))) for i in range(20)]}
        self.free_dsems = {k: list(v) for k, v in self.all_dsems.items()}
        self.phase_bufs = []

    def buf(self, name):
        b = Buf(name)
        self.phase_bufs.append(b)
        return b

    def bufs(self, name, n):
        return [self.buf("%s%d" % (name, i)) for i in range(n)]

    def _need(self, E, ev):
        sem, val = ev
        if E.waited.get(id(sem), 0) >= val:
            return
        E.waited[id(sem)] = val
        E.ops.append(lambda e, s=sem.h, v=val: e.wait_ge(s, v))

    def op(self, en, fn, reads=(), writes=()):
        E = self.engs[en]
        for b in reads:
            if b.w is not None:
                self._need(E, b.w)
        for b in writes:
            if b.w is not None and b.w[0] is not E.sem:
                self._need(E, b.w)
            for ev in b.rs.values():
                if ev[0] is not E.sem:
                    self._need(E, ev)
        E.sem.count += 1
        ev = (E.sem, E.sem.count)
        E.ops.append(lambda e, f=fn, s=E.sem.h: f(e).then_inc(s, 1))
        for b in writes:
            b.w = ev
            b.rs = {}
        for b in reads:
            b.rs[id(E.sem)] = ev
        return ev

    def _dsem(self, b, qn):
        if b.dsem is None:
            b.dsem = (qn, self.free_dsems[qn].pop())
        assert b.dsem[0] == qn, "buffer %s is DMA'd from two queue kinds" % b.name
        return b.dsem[1]

    def dma(self, qn, out_ap, in_ap, src=None, dst=None, sem=None):
        Q = self.engs[qn]
        if src is not None and src.w is not None:
            self._need(Q, src.w)
        if dst is not None:
            if dst.w is not None:
                self._need(Q, dst.w)
            for ev in dst.rs.values():
                self._need(Q, ev)
        S = sem if sem is not None else self._dsem(dst if dst is not None else src, qn)
        S.count += 16
        ev = (S, S.count)
        Q.ops.append(lambda e, o=out_ap, i=in_ap, s=S.h: e.dma_start(out=o, in_=i).then_inc(s, 16))
        if dst is not None:
            dst.w = ev
            dst.rs = {}
        if src is not None:
            src.rs[id(S)] = ev
        if qn == "pool" and sem is None and self.hook is not None:
            self._npool += 1
            if self._npool % self.hook_every == 0:
                self.hook()
        return ev

    def wait_sem(self, en, S):
        self._need(self.engs[en], (S, S.count))

    def end_phase(self, extra_sems=()):
        sems = [E.sem for E in self.engs.values() if E.sem.count > 0]
        sems += [s for v in self.all_dsems.values() for s in v if s.count > 0] + [s for s in extra_sems if s.count > 0]
        for E in self.engs.values():
            for s in sems:
                if s is not E.sem:
                    self._need(E, (s, s.count))
        with self.nc.Block() as block:
            def mk(E):
                def body(e):
                    for f in E.ops:
                        f(e)
                return body
            block.tensor(mk(self.engs["pe"]))
            block.scalar(mk(self.engs["act"]))
            block.vector(mk(self.engs["dve"]))
            block.gpsimd(mk(self.engs["pool"]))
            block.sync(mk(self.engs["sp"]))
        for E in self.engs.values():
            E.ops = []
        for b in self.phase_bufs:
            b.dsem = None
        self.phase_bufs = []
        self.free_dsems = {k: list(v) for k, v in self.all_dsems.items()}
        self.nph += 1
        if self.max_ph is not None and self.nph >= self.max_ph:
            raise _Stop()


class WStream:
    def __init__(self, pg, tiles, bufs, blocks, presems):
        self.pg, self.tiles, self.bufs, self.blocks, self.presems = pg, tiles, bufs, blocks, presems
        self.nload = 0
        self.nget = 0

    def _load(self):
        k = self.nload
        if k >= len(self.blocks):
            return
        ap, shape_sel, psem = self.blocks[k]
        i = k % len(self.bufs)
        if psem is not None:
            self.pg.wait_sem("sp", psem)
        self.pg.dma("sp", shape_sel(self.tiles[i]), ap, dst=self.bufs[i])
        self.nload += 1

    def get(self):
        k = self.nget
        while self.nload < min(k + len(self.bufs), len(self.blocks)):
            self._load()
        self.nget += 1
        i = k % len(self.bufs)
        return self.tiles[i], self.bufs[i]

    def prefetch(self):
        while self.nload < min(self.nget + len(self.bufs), len(self.blocks)):
            self._load()


def build_program(cfg):
    DEPTH, NTP, T = cfg["DEPTH"], cfg["NTP"], cfg["T"]
    NT = NTP + NS
    MT = NTP // 128 + 1
    NA = (DEPTH + 1) // 2
    NB = DEPTH // 2
    assert NTP % 128 == 0

    nc = bass.Bass("TRN2", target_bir_lowering=False)

    uid = [0]

    def sbt(name, shape, dt):
        uid[0] += 1
        return nc.sbuf_tensor("%s_%d" % (name, uid[0]), shape, dt)

    def pst(name, shape, dt):
        uid[0] += 1
        return nc.psum_tensor("%s_%d" % (name, uid[0]), shape, dt)

    def din(name, shape, dt=F32):
        return nc.dram_tensor(name, list(shape), dt, kind="ExternalInput")

    def dout(name, shape):
        return nc.dram_tensor(name, list(shape), F32, kind="ExternalOutput")

    def dscr(name, shape, dt):
        return nc.dram_tensor(name, list(shape), dt, kind="Internal")

    x_p = din("x_p", [NTP, D]).ap()
    x_s = din("x_s", [NS, D]).ap()
    ca_k = din("ca_k", [NA, A_CL, D]).ap()
    ca_v = din("ca_v", [NA, A_CL, D]).ap()
    cb_k = din("cb_k", [max(NB, 1), B_CL, 512]).ap()
    cb_v = din("cb_v", [max(NB, 1), B_CL, 512]).ap()
    gains = din("gains", [128, 2 * DEPTH * KC]).ap()
    gfin = din("gfin", [D])
    relext = din("relext", [NA, A_H * 768])
    sinks = din("sinks", [max(NB, 1), B_HQ])
    w_aqkv = din("w_aqkv", [NA, D, 3 * D]).ap()
    w_ao = din("w_ao", [NA, D, D]).ap()
    w_bqkv = din("w_bqkv", [max(NB, 1), D, 3072]).ap()
    w_bo = din("w_bo", [max(NB, 1), D, D]).ap()
    w_up = din("w_up", [DEPTH, D, DFF]).ap()
    w_dn = din("w_dn", [DEPTH, DFF, D]).ap()

    y_p = dout("y_p", [NTP, D]).ap()
    y_s = dout("y_s", [NS, D]).ap()
    sa_kp = dout("sa_kp", [NA, A_CL, D]).ap()
    sa_vp = dout("sa_vp", [NA, A_CL, D]).ap()
    sb_kp = dout("sb_kp", [max(NB, 1), B_CL, 512]).ap()
    sb_vp = dout("sb_vp", [max(NB, 1), B_CL, 512]).ap()
    sa_ks = dout("sa_ks", [NA, A_CL, D]).ap()
    sa_vs = dout("sa_vs", [NA, A_CL, D]).ap()
    sb_ks = dout("sb_ks", [max(NB, 1), B_CL, 512]).ap()
    sb_vs = dout("sb_vs", [max(NB, 1), B_CL, 512]).ap()

    xs = dscr("xs", [128, KC, NT], F32).ap()
    qT = dscr("qT", [D, NT], BF16).ap()
    kT = dscr("kT", [D, NT], BF16).ap()
    oT = dscr("oT", [D, NT], BF16).ap()
    v_a = dscr("v_a", [A_H, 128, MT, A_DH], BF16).ap()
    v_b = dscr("v_b", [B_HKV, 128, MT, B_DH], BF16).ap()
    zt = [dscr("zt%d" % i, [128, A_H * 768], F32) for i in range(NA)]
    wq_s = [dscr("wq_s%d" % i, [12, 128, KC, 512], BF16).ap() for i in range(2)]
    wo_s = [dscr("wo_s%d" % i, [4, 128, KC, 512], BF16).ap() for i in range(2)]
    wu_s = [dscr("wu_s%d" % i, [16, 128, KC, 512], BF16).ap() for i in range(2)]
    wd_s = [dscr("wd_s%d" % i, [16, 128, 64, 128], BF16).ap() for i in range(2)]

    blocks = []
    t0 = 0
    while t0 < NTP:
        n = min(T, NTP - t0)
        blocks.append([t0, n])
        t0 += n
    if blocks[-1][1] + NS <= 512 and blocks[-1][1] <= 256:
        blocks[-1][1] += NS
    else:
        blocks.append([NTP, NS])

    try:
        _build_body(nc, cfg, locals())
    except _Stop:
        pass
    return nc


def _build_body(nc, cfg, L):
    globals_needed = L
    DEPTH, NTP, T = cfg["DEPTH"], cfg["NTP"], cfg["T"]
    NT, MT, NA, NB = L["NT"], L["MT"], L["NA"], L["NB"]
    sbt, pst, blocks = L["sbt"], L["pst"], L["blocks"]
    (x_p, x_s, ca_k, ca_v, cb_k, cb_v, gains, gfin, relext, sinks, w_aqkv, w_ao, w_bqkv, w_bo, w_up, w_dn) = [
        L[k] for k in "x_p x_s ca_k ca_v cb_k cb_v gains gfin relext sinks w_aqkv w_ao w_bqkv w_bo w_up w_dn".split()]
    (y_p, y_s, sa_kp, sa_vp, sb_kp, sb_vp, sa_ks, sa_vs, sb_ks, sb_vs) = [
        L[k] for k in "y_p y_s sa_kp sa_vp sb_kp sb_vp sa_ks sa_vs sb_ks sb_vs".split()]
    (xs, qT, kT, oT, v_a, v_b, zt, wq_s, wo_s, wu_s, wd_s) = [
        L[k] for k in "xs qT kT oT v_a v_b zt wq_s wo_s wu_s wd_s".split()]
    with contextlib.ExitStack() as es:
        pg = Prog(nc, es)
        pg.max_ph = cfg.get("NPH")
        SKIP = cfg.get("SKIP", ())
        castsems = {(k, par): Sem(es.enter_context(nc.semaphore("c_%s%d" % (k, par))))
                    for k in ("q", "o", "u", "d") for par in range(2)}
        miscsem = Sem(es.enter_context(nc.semaphore("misc")))
        gsem = Sem(es.enter_context(nc.semaphore("gsem")))

        ident = es.enter_context(sbt("ident", [128, 128], F32))
        ones_m = es.enter_context(sbt("ones_m", [128, 128], BF16))
        ones_b = es.enter_context(sbt("ones_b", [128, 128], BF16))
        gn = es.enter_context(sbt("gn", [128, 2 * DEPTH * KC], F32))
        c_ident, c_ones, c_gn = Buf("ident"), Buf("ones"), Buf("gn")

        def emit_consts():
            pg.op("pool", lambda e: e.memset(ident[:], 0.0), writes=[c_ident])
            pg.op("pool", lambda e: e.affine_select(out=ident[:], in_=ident[:], pattern=[[-1, 128]],
                                                    compare_op=ALU.not_equal, fill=1.0, base=0,
                                                    channel_multiplier=1), reads=[c_ident], writes=[c_ident])

            def f2(e):
                e.memset(ones_m[:], 1.0 / D)
                return e.memset(ones_b[:], 1.0)
            pg.op("pool", f2, writes=[c_ones])
            pg.dma("sp", gn[:], gains, dst=c_gn, sem=gsem)

        pending = []

        def emit_casts(layer):
            if "cast" in SKIP:
                return
            par = layer % 2
            slot = layer // 2
            isA = (layer % 2 == 0)
            wqkv = (w_aqkv if isA else w_bqkv)[slot]
            wo = (w_ao if isA else w_bo)[slot]
            nq = 12 if isA else 6
            for b in range(nq):
                src = wqkv[:, b * 512:(b + 1) * 512].rearrange("(kc p) n -> p kc n", p=128)
                pending.append((wq_s[par][b], src, castsems[("q", par)], layer, "q"))
            for b in range(4):
                src = wo[:, b * 512:(b + 1) * 512].rearrange("(kc p) n -> p kc n", p=128)
                pending.append((wo_s[par][b], src, castsems[("o", par)], layer, "o"))
            for b in range(16):
                src = w_up[layer][:, b * 512:(b + 1) * 512].rearrange("(kc p) n -> p kc n", p=128)
                pending.append((wu_s[par][b], src, castsems[("u", par)], layer, "u"))
            for b in range(16):
                src = w_dn[layer][:, b * 128:(b + 1) * 128].rearrange("(kc p) n -> p kc n", p=128)
                pending.append((wd_s[par][b], src, castsems[("d", par)], layer, "d"))

        def drip(n=1):
            for _ in range(n):
                if not pending:
                    return
                dst_, src_, sem_, _l, _k = pending.pop(0)
                pg.dma("pool", dst_, src_, sem=sem_)
        pg.hook = drip

        def flush_casts(layer, kinds):
            while pending and pending[0][3] <= layer and (pending[0][3] < layer or pending[0][4] in kinds):
                drip(1)

        with contextlib.ExitStack() as ph:
            xin = [ph.enter_context(sbt("xin%d" % i, [128, D], F32)) for i in range(2)]
            xst = [ph.enter_context(sbt("xst%d" % i, [128, KC, 128], F32)) for i in range(2)]
            extb = ph.enter_context(sbt("extb", [128, A_H * 768], F32))
            ps = ph.enter_context(pst("ps0", [128, 4096], F32))
            b_xin, b_xst, b_ps = pg.bufs("xin", 2), pg.bufs("xst", 2), pg.bufs("ps", 8)
            b_ext = pg.buf("extb")
            emit_consts()
            for s in range(NA):
                pg.dma("pool", sa_ks[s, 0:A_CL - NS, :], ca_k[s, NS:A_CL, :], sem=miscsem)
                pg.dma("pool", sa_vs[s, 0:A_CL - NS, :], ca_v[s, NS:A_CL, :], sem=miscsem)
            for s in range(NB):
                pg.dma("pool", sb_ks[s, 0:B_CL - NS, :], cb_k[s, NS:B_CL, :], sem=miscsem)
                pg.dma("pool", sb_vs[s, 0:B_CL - NS, :], cb_v[s, NS:B_CL, :], sem=miscsem)
            emit_casts(0)
            drip(12)
            pg.hook_every = 4
            for s in range(NA):
                src = bass.AP(relext, s * A_H * 768, [[0, 128], [1, A_H * 768]])
                pg.dma("sp", extb[:], src, dst=b_ext)
                pg.dma("sp", zt[s].ap(), extb[:], src=b_ext)
            pscnt = 0
            for tt in range(MT):
                rows = 128 if tt < MT - 1 else NS
                src = x_p[tt * 128:(tt + 1) * 128, :] if tt < MT - 1 else x_s
                i = tt % 2
                pg.dma("sp", xin[i][0:rows, :], src, dst=b_xin[i])
                for k in range(4):
                    bi = pscnt % 8
                    pscnt += 1

                    def f(e, i=i, k=k, bi=bi, rows=rows):
                        for j in range(4):
                            c = 4 * k + j
                            r = e.transpose(ps[:, bi * 512 + j * 128: bi * 512 + j * 128 + rows],
                                            xin[i][0:rows, c * 128:(c + 1) * 128], ident[0:rows, 0:rows])
                        return r
                    pg.op("pe", f, reads=[b_xin[i], c_ident], writes=[b_ps[bi]])
                    src_ps = ps[:, bi * 512:(bi + 1) * 512].rearrange("p (j t) -> p j t", j=4)[:, :, 0:rows]
                    dstv = xst[i][:, 4 * k:4 * k + 4, 0:rows]
                    if k % 2 == 0:
                        pg.op("act", lambda e, o=dstv, s_=src_ps: e.copy(out=o, in_=s_),
                              reads=[b_ps[bi]], writes=[b_xst[i]])
                    else:
                        pg.op("dve", lambda e, o=dstv, s_=src_ps: e.tensor_copy(out=o, in_=s_),
                              reads=[b_ps[bi]], writes=[b_xst[i]])
                pg.dma("pool", xs[:, :, tt * 128: tt * 128 + rows], xst[i][:, :, 0:rows], src=b_xst[i])
            pg.end_phase([miscsem, gsem])

        for layer in range(DEPTH):
            isA = (layer % 2 == 0)
            slot = layer // 2
            par = layer % 2
            nqb = 12 if isA else 6
            CL = A_CL if isA else B_CL
            g_mix = gn[:, layer * KC:(layer + 1) * KC]
            g_ffn = gn[:, (DEPTH + layer) * KC:(DEPTH + layer + 1) * KC]
            k_st_p = (sa_kp if isA else sb_kp)[slot]
            v_st_p = (sa_vp if isA else sb_vp)[slot]
            k_st_s = (sa_ks if isA else sb_ks)[slot]
            v_st_s = (sa_vs if isA else sb_vs)[slot]

            def rmsnorm(ph_t, xb_t, b_xb, n, hT_t, b_hT, g_ap, sq, b_sq, psr, b_psr, rs, b_rs):
                for c in range(KC):
                    i = c % 2
                    pg.op("act", lambda e, c=c, i=i: e.activation(out=sq[i][:, 0:n], in_=xb_t[:, c, 0:n],
                                                                  func=AF.Square),
                          reads=[b_xb], writes=[b_sq[i]])
                    pg.op("pe", lambda e, c=c, i=i: e.matmul(psr[:, 0:n], ones_m[:], sq[i][:, 0:n],
                                                             start=(c == 0), stop=(c == KC - 1)),
                          reads=[b_sq[i], c_ones], writes=[b_psr])
                pg.op("dve", lambda e: e.tensor_scalar(out=rs[:, 0:n], in0=psr[:, 0:n], scalar1=EPS, scalar2=None,
                                                       op0=ALU.add), reads=[b_psr], writes=[b_rs])
                pg.op("act", lambda e: e.sqrt(out=rs[:, 0:n], in_=rs[:, 0:n]), reads=[b_rs], writes=[b_rs])
                pg.op("dve", lambda e: e.reciprocal(out=rs[:, 0:n], in_=rs[:, 0:n]), reads=[b_rs], writes=[b_rs])
                for c in range(KC):
                    pg.op("dve", lambda e, c=c: e.scalar_tensor_tensor(
                        out=hT_t[:, c, 0:n], in0=xb_t[:, c, 0:n], scalar=g_ap[:, c:c + 1], in1=rs[:, 0:n],
                        op0=ALU.mult, op1=ALU.mult), reads=[b_xb, b_rs, c_gn], writes=[b_hT])

            with contextlib.ExitStack() as ph:
                xb = [ph.enter_context(sbt("xb%d" % i, [128, KC, 512], F32)) for i in range(2)]
                hT = [ph.enter_context(sbt("hT%d" % i, [128, KC, 512], BF16)) for i in range(2)]
                wt = [ph.enter_context(sbt("wt%d" % i, [128, KC, 512], BF16)) for i in range(3)]
                sq = [ph.enter_context(sbt("sq%d" % i, [128, 512], BF16)) for i in range(2)]
                rs = ph.enter_context(sbt("rs", [128, 512], F32))
                stg = [ph.enter_context(sbt("stg%d" % i, [128, 512], BF16)) for i in range(6)]
                vstg = [ph.enter_context(sbt("vstg%d" % i, [128, 512], BF16)) for i in range(3)]
                sstg = [ph.enter_context(sbt("sstg%d" % i, [128, 512], F32)) for i in range(3)]
                ps = ph.enter_context(pst("psA", [128, 4096], F32))
                b_xb, b_hT, b_wt, b_sq = pg.bufs("xb", 2), pg.bufs("hT", 2), pg.bufs("wt", 3), pg.bufs("sq", 2)
                b_rs, b_stg, b_vstg, b_sstg = pg.buf("rs"), pg.bufs("stg", 6), pg.bufs("vstg", 3), pg.bufs("sstg", 3)
                b_ps = pg.bufs("ps", 8)
                flush_casts(layer, "q")
                pg.hook_every = 6 if layer == 0 else 40
                if layer + 1 < DEPTH and 'nextcast' not in SKIP:
                    emit_casts(layer + 1)
                wblocks = []
                for _blk in blocks:
                    for b in range(nqb):
                        wblocks.append((wq_s[par][b], (lambda t: t[:]), castsems[("q", par)]))
                ws = WStream(pg, wt, b_wt, wblocks, None)
                cnt = dict(ps=1, stg=0, vstg=0, sstg=0)

                def nextps():
                    bi = 1 + (cnt["ps"] - 1) % 7
                    cnt["ps"] += 1
                    return bi

                def prep(bidx):
                    if bidx >= len(blocks):
                        return
                    t0_, n_ = blocks[bidx]
                    xi_ = bidx % 2
                    pg.dma("sp", xb[xi_][:, :, 0:n_], xs[:, :, t0_:t0_ + n_], dst=b_xb[xi_])
                    rmsnorm(ph, xb[xi_], b_xb[xi_], n_, hT[xi_], b_hT[xi_], g_mix, sq, b_sq,
                            ps[:, 0:512], b_ps[0], rs, b_rs)

                prep(0)
                for bidx, (t0, n) in enumerate(blocks):
                    xi = bidx % 2
                    tiles = []
                    c0 = 0
                    while c0 < n:
                        tok = t0 + c0
                        if tok >= NTP:
                            tiles.append((c0, NS, "s", CL - NS, tok))
                            c0 += NS
                        else:
                            r = min(128, NTP - tok, n - c0)
                            srow = tok - (NTP - CL) if tok >= NTP - CL else None
                            tiles.append((c0, r, "p", srow, tok))
                            c0 += r
                    for b in range(nqb):
                        W, bW = ws.get()
                        if isA:
                            kind = "q" if b < 4 else ("k" if b < 8 else "v")
                            cb = b % 4
                        else:
                            kind = "q" if b < 4 else ("k" if b == 4 else "v")
                            cb = b if b < 4 else 0
                        if kind in ("q", "k") and "qk" not in SKIP:
                            dstT = qT if kind == "q" else kT
                            for j in range(4):
                                bi = nextps()

                                def f(e, j=j, bi=bi, W=W, xi=xi, n=n):
                                    for kc in range(KC):
                                        r = e.matmul(ps[:, bi * 512: bi * 512 + n], W[:, kc, j * 128:(j + 1) * 128],
                                                     hT[xi][:, kc, 0:n], start=(kc == 0), stop=(kc == KC - 1))
                                    return r
                                pg.op("pe", f, reads=[bW, b_hT[xi]], writes=[b_ps[bi]])
                                si = cnt["stg"] % 6
                                cnt["stg"] += 1
                                pg.op("act", lambda e, si=si, bi=bi, n=n: e.copy(out=stg[si][:, 0:n],
                                                                                  in_=ps[:, bi * 512: bi * 512 + n]),
                                      reads=[b_ps[bi]], writes=[b_stg[si]])
                                r0 = cb * 512 + j * 128
                                pg.dma("sp", dstT[r0:r0 + 128, t0:t0 + n], stg[si][:, 0:n], src=b_stg[si])
                        if kind in ("k", "v") and "kv" not in SKIP:
                            for (c0, r, tk, srow, tok) in tiles:
                                if kind == "k" and srow is None:
                                    continue
                                bi = nextps()

                                def f(e, bi=bi, W=W, xi=xi, c0=c0, r=r):
                                    for kc in range(KC):
                                        rr = e.matmul(ps[0:r, bi * 512:(bi + 1) * 512], hT[xi][:, kc, c0:c0 + r],
                                                      W[:, kc, :], start=(kc == 0), stop=(kc == KC - 1))
                                    return rr
                                pg.op("pe", f, reads=[bW, b_hT[xi]], writes=[b_ps[bi]])
                                if kind == "v":
                                    vi = cnt["vstg"] % 3
                                    cnt["vstg"] += 1
                                    pg.op("dve", lambda e, vi=vi, bi=bi, r=r: e.tensor_copy(
                                        out=vstg[vi][0:r, :], in_=ps[0:r, bi * 512:(bi + 1) * 512]),
                                        reads=[b_ps[bi]], writes=[b_vstg[vi]])
                                    m = tok // 128
                                    if isA:
                                        dst = v_a[4 * cb:4 * cb + 4, 0:r, m, :].rearrange("h p d -> p h d")
                                        srcv = vstg[vi][0:r, :].rearrange("p (h d) -> p h d", h=4)
                                    else:
                                        dst = v_b[:, 0:r, m, :].rearrange("h p d -> p h d")
                                        srcv = vstg[vi][0:r, :].rearrange("p (h d) -> p h d", h=8)
                                    if "vstore" not in SKIP:
                                        pg.dma("pool", dst, srcv, src=b_vstg[vi])
                                if srow is not None:
                                    ssi = cnt["sstg"] % 3
                                    cnt["sstg"] += 1
                                    if kind == "v":
                                        pg.op("dve", lambda e, ssi=ssi, bi=bi, r=r: e.tensor_copy(
                                            out=sstg[ssi][0:r, :], in_=ps[0:r, bi * 512:(bi + 1) * 512]),
                                            reads=[b_ps[bi]], writes=[b_sstg[ssi]])
                                    else:
                                        pg.op("act", lambda e, ssi=ssi, bi=bi, r=r: e.copy(
                                            out=sstg[ssi][0:r, :], in_=ps[0:r, bi * 512:(bi + 1) * 512]),
                                            reads=[b_ps[bi]], writes=[b_sstg[ssi]])
                                    if tk == "p":
                                        dsto = (k_st_p if kind == "k" else v_st_p)
                                    else:
                                        dsto = (k_st_s if kind == "k" else v_st_s)
                                    if "sstore" not in SKIP:
                                        pg.dma("pool", dsto[srow:srow + r, cb * 512:(cb + 1) * 512],
                                               sstg[ssi][0:r, :], src=b_sstg[ssi])
                        ws.prefetch()
                        if b == 1:
                            prep(bidx + 1)
                pg.end_phase([miscsem, gsem])

            with contextlib.ExitStack() as ph:
                if isA:
                    NG, G, DH, NSL, H_PART = A_H, 1, 128, 5, 128
                else:
                    NG, G, DH, NSL, H_PART = B_HKV, 4, 64, 2, 64
                SW = G * 128
                NCOL = NSL * SW
                scale = float(DH) ** -0.5
                kq = [ph.enter_context(sbt("k_t%d" % i, [128, NT], BF16)) for i in range(2)]
                QB = NT if isA else (1024 + NS)
                qq = [ph.enter_context(sbt("q_t%d" % i, [128, G, QB], BF16)) for i in range(2)]
                vv = [ph.enter_context(sbt("v_t%d" % i, [128, MT, DH], BF16)) for i in range(2)]
                ost = [ph.enter_context(sbt("ost%d" % i, [128, G, QB], BF16)) for i in range(2)]
                EB = ph.enter_context(sbt("EB", [128, NG, NCOL], F32))
                pf = [ph.enter_context(sbt("pf%d" % i, [128, 1024], F32)) for i in range(2)]
                pb = [ph.enter_context(sbt("pb%d" % i, [128, 1024], BF16)) for i in range(2)]
                rd = [ph.enter_context(sbt("rd%d" % i, [128, 512], F32)) for i in range(2)]
                kcs = [ph.enter_context(sbt("kcs%d" % i, [128, D if isA else 512], F32)) for i in range(2)]
                kcT = ph.enter_context(sbt("kcT", [128, NG, CL], BF16))
                vc = ph.enter_context(sbt("vc", [128, CL // 128, D if isA else 512], BF16))
                ps = ph.enter_context(pst("psB", [128, 4096], F32))
                b_kq, b_qq, b_vv, b_ost = pg.bufs("kq", 2), pg.bufs("qq", 2), pg.bufs("vv", 2), pg.bufs("ost", 2)
                b_EB, b_rd = pg.buf("EB"), pg.bufs("rd", 2)
                b_pf = [pg.bufs("pf%d_" % i, 2) for i in range(2)]
                b_pb = [pg.bufs("pb%d_" % i, 2) for i in range(2)]
                b_kcs, b_kcT, b_vc = pg.bufs("kcs", 2), pg.buf("kcT"), pg.buf("vc")
                b_sps, b_acc, b_den = pg.bufs("sps", 2), pg.bufs("acc", 2), pg.bufs("den", 2)
                sps = [ps[:, 0:1024], ps[:, 1024:2048]]
                acc = [ps[:, 2048:2560], ps[:, 2560:3072]]
                den = [ps[:, 3072:3584], ps[:, 3584:4096]]
                cache_k = (ca_k if isA else cb_k)[slot]
                cache_v = (ca_v if isA else cb_v)[slot]
                FW = D if isA else 512

                if isA:
                    EBv = EB[:].rearrange("p h (s q) -> p h s q", s=NSL)
                    for s in range(NSL):
                        off = 639 - 128 * s
                        src = bass.AP(zt[slot], off, [[A_H * 768 - 1, 128], [768, A_H], [1, 128]])
                        pg.dma("sp", EBv[:, :, s, :], src, dst=b_EB)
                    for h0 in range(0, NG, 4):
                        pg.op("act", lambda e, h0=h0: e.activation(out=EB[:, h0:h0 + 4, :], in_=EB[:, h0:h0 + 4, :],
                                                                   func=AF.Exp), reads=[b_EB], writes=[b_EB])

                    def fm(e):
                        e.memset(EBv[0:64, :, 0, 64:128], 0.0)
                        return e.memset(EBv[64:128, :, NSL - 1, 0:64], 0.0)
                    pg.op("pool", fm, reads=[b_EB], writes=[b_EB])
                else:
                    esk = ph.enter_context(sbt("esk", [128, B_HQ], F32))
                    esb = ph.enter_context(sbt("esb", [64, B_HKV, 512], F32))
                    iot = ph.enter_context(sbt("iot", [128, 1024], F32))
                    b_esk, b_esb, b_iot = pg.buf("esk"), pg.buf("esb"), pg.buf("iot")
                    pg.dma("sp", esk[:], bass.AP(sinks, slot * B_HQ, [[0, 128], [1, B_HQ]]), dst=b_esk)
                    pg.op("act", lambda e: e.activation(out=esk[:], in_=esk[:], func=AF.Exp),
                          reads=[b_esk], writes=[b_esk])

                    pg.op("pool", lambda e: e.iota(iot[:], pattern=[[128, 2], [0, 4], [-1, 128]], base=-128,
                                                   channel_multiplier=1, allow_small_or_imprecise_dtypes=True),
                          writes=[b_iot])
                    pg.op("pool", lambda e: e.memset(esb[:], 0.0), writes=[b_esb])
                    pg.op("act", lambda e: e.activation(out=iot[:], in_=iot[:], func=AF.Abs),
                          reads=[b_iot], writes=[b_iot])
                    EBv = EB[:].rearrange("p h (s g q) -> p h s g q", s=2, g=4)
                    iov = iot[:].rearrange("p (s g q) -> p s g q", s=2, g=4)
                    for kvh in range(B_HKV):
                        for g in range(B_G):
                            hq = kvh * B_G + g
                            slope = 2.0 ** (-8.0 * (hq + 1) / B_HQ)
                            pg.op("act", lambda e, kvh=kvh, g=g, slope=slope: e.activation(
                                out=EBv[:, kvh, :, g, :], in_=iov[:, :, g, :], func=AF.Exp, scale=-slope),
                                reads=[b_iot], writes=[b_EB])
                            pg.op("dve", lambda e, kvh=kvh, g=g, hq=hq: e.tensor_scalar(
                                out=esb[:, kvh, g * 128:(g + 1) * 128], in0=esb[:, kvh, g * 128:(g + 1) * 128],
                                scalar1=esk[0:64, hq:hq + 1], scalar2=None, op0=ALU.add),
                                reads=[b_esk, b_esb], writes=[b_esb])

                    es_hi = ph.enter_context(sbt("es_hi", [1, B_HKV, 512], BF16))
                    es_lo = ph.enter_context(sbt("es_lo", [1, B_HKV, 512], BF16))
                    b_eshl = pg.buf("eshl")
                    pg.op("dve", lambda e: e.tensor_copy(out=es_hi[:], in_=esb[0:1, :, :]), reads=[b_esb], writes=[b_eshl])
                    pg.op("dve", lambda e: e.tensor_tensor(out=es_lo[:], in0=esb[0:1, :, :], in1=es_hi[:],
                                                           op=ALU.subtract), reads=[b_esb, b_eshl], writes=[b_eshl])

                    def fm(e):
                        for kvh in range(B_HKV):
                            e.memset(EBv[0:64, kvh, 0, :, 64:128], 0.0)
                            r = e.memset(EBv[64:128, kvh, 1, :, 0:64], 0.0)
                        return r
                    pg.op("pool", fm, reads=[b_EB], writes=[b_EB])

                pg.dma("pool", vc[:], cache_v.rearrange("(s p) f -> p s f", p=128), dst=b_vc)
                tcnt = 0
                for s in range(CL // 128):
                    ki = s % 2
                    pg.dma("sp", kcs[ki][:], cache_k[s * 128:(s + 1) * 128, :], dst=b_kcs[ki])
                    for h0 in range(0, NG, 4):
                        si = tcnt % 2
                        tcnt += 1

                        def f(e, ki=ki, h0=h0, si=si):
                            for j in range(4):
                                h = h0 + j
                                r = e.transpose(sps[si][0:DH, j * 128:(j + 1) * 128],
                                                kcs[ki][:, h * DH:(h + 1) * DH], ident[:])
                            return r
                        pg.op("pe", f, reads=[b_kcs[ki], c_ident], writes=[b_sps[si]])
                        pg.op("dve", lambda e, h0=h0, si=si, s=s: e.tensor_copy(
                            out=kcT[0:DH, h0:h0 + 4, s * 128:(s + 1) * 128],
                            in_=sps[si][0:DH, 0:512].rearrange("p (j k) -> p j k", j=4)),
                            reads=[b_sps[si]], writes=[b_kcT])

                npairs = NTP // 128
                PPB = npairs if isA else 8
                sbs = []
                items = []
                for gi in range(NG):
                    p0 = 0
                    while p0 < npairs:
                        p1 = min(npairs, p0 + PPB)
                        last = (p1 == npairs)
                        sbi = len(sbs)
                        sbs.append((gi, p0 * 128, (p1 - p0) * 128 + (NS if last else 0)))
                        for p in range(p0, p1):
                            items.append((gi, p, sbi, p == p0, (p == p1 - 1) and not last))
                        if last:
                            items.append((gi, "S", sbi, False, True))
                        p0 = p1
                assert max(x[2] for x in sbs) <= QB

                def load_kv(gi):
                    i = gi % 2
                    if isA:
                        pg.dma("sp", kq[i][:], kT[gi * 128:(gi + 1) * 128, :], dst=b_kq[i])
                        pg.dma("sp", vv[i][:, 0:MT - 1, :], v_a[gi][:, 0:MT - 1, :], dst=b_vv[i])
                        pg.dma("sp", vv[i][0:NS, MT - 1, :], v_a[gi][0:NS, MT - 1, :], dst=b_vv[i])
                    else:
                        pg.dma("sp", kq[i][0:64, :], kT[gi * 64:(gi + 1) * 64, :], dst=b_kq[i])
                        pg.dma("sp", vv[i][:, 0:MT - 1, :], v_b[gi][:, 0:MT - 1, :], dst=b_vv[i])
                        pg.dma("sp", vv[i][0:NS, MT - 1, :], v_b[gi][0:NS, MT - 1, :], dst=b_vv[i])

                def load_q(sbi):
                    if sbi >= len(sbs):
                        return
                    gi, tok0, ntok = sbs[sbi]
                    j = sbi % 2
                    if isA:
                        pg.dma("sp", qq[j][:, 0, 0:ntok], qT[gi * 128:(gi + 1) * 128, tok0:tok0 + ntok], dst=b_qq[j])
                    else:
                        pg.dma("sp", qq[j][0:64, :, 0:ntok],
                               qT[gi * 256:(gi + 1) * 256, tok0:tok0 + ntok].rearrange("(g d) t -> d g t", g=4),
                               dst=b_qq[j])

                def store_o(sbi):
                    gi, tok0, ntok = sbs[sbi]
                    j = sbi % 2
                    if isA:
                        pg.dma("pool", oT[gi * 128:(gi + 1) * 128, tok0:tok0 + ntok], ost[j][:, 0, 0:ntok],
                               src=b_ost[j])
                    else:
                        pg.dma("pool",
                               oT[gi * 256:(gi + 1) * 256, tok0:tok0 + ntok].rearrange("(g d) t -> d g t", g=4),
                               ost[j][0:64, :, 0:ntok], src=b_ost[j])

                def slots_of(it):
                    gi, p = it[0], it[1]
                    i = gi % 2
                    out = []
                    if p == "S":
                        for s in range(NSL - 1):
                            out.append((s, 128, kcT[0:DH, gi, s * 128:(s + 1) * 128],
                                        vc[:, s, gi * DH:(gi + 1) * DH]))
                        out.append((NSL - 1, NS, kq[i][0:DH, NTP:NTP + NS], vv[i][0:NS, MT - 1, :]))
                    else:
                        for s in range(NSL):
                            m = p - (NSL - 1) + s
                            if m < 0:
                                continue
                            out.append((s, 128, kq[i][0:DH, m * 128:(m + 1) * 128], vv[i][:, m, :]))
                    return out

                def stage1(idx):
                    it = items[idx]
                    gi, p, sbi, first, lastf = it
                    i = gi % 2
                    j = sbi % 2
                    bi = idx % 2
                    sl = slots_of(it)
                    Q = NS if p == "S" else 128
                    q0 = (NTP if p == "S" else p * 128) - sbs[sbi][1]
                    if first:
                        load_q(sbi + 1)
                    dl = []

                    def f(e):
                        for (s, nk, kap, vap) in sl:
                            for g in range(G):
                                c = s * SW + g * 128
                                r = e.matmul(sps[bi][0:nk, c:c + Q], kap, qq[j][0:DH, g, q0:q0 + Q],
                                             start=True, stop=True)
                        return r
                    pg.op("pe", f, reads=[b_kq[i], b_qq[j], b_kcT], writes=[b_sps[bi]])
                    if p == "S":
                        for (s, nk, kap, vap) in sl:
                            def v3(t, s=s, nk=nk):
                                return t[0:nk, s * SW:(s + 1) * SW].rearrange("p (g q) -> p g q", g=G)[:, :, 0:NS]
                            sg = (s * SW) // 512
                            pg.op("act", lambda e, v3=v3: e.activation(out=v3(pf[bi]), in_=v3(sps[bi]), func=AF.Exp,
                                                                       scale=scale),
                                  reads=[b_sps[bi]], writes=[b_pf[bi][sg]])
                            dl.append((lambda e, v3=v3: e.tensor_tensor(out=v3(pb[bi]), in0=v3(pf[bi]),
                                                                        in1=v3(EB[:, gi, :]), op=ALU.mult),
                                       [b_pf[bi][sg], b_EB], [b_pb[bi][sg]], "dve"))
                    else:
                        c0 = sl[0][0] * SW
                        a = c0
                        while a < NCOL:
                            bnd = min(NCOL, (a // 512 + 1) * 512)
                            pg.op("act", lambda e, a=a, bnd=bnd: e.activation(out=pf[bi][:, a:bnd],
                                                                              in_=sps[bi][:, a:bnd], func=AF.Exp,
                                                                              scale=scale),
                                  reads=[b_sps[bi]], writes=[b_pf[bi][a // 512]])
                            dl.append((lambda e, a=a, bnd=bnd: e.tensor_tensor(out=pb[bi][:, a:bnd],
                                                                               in0=pf[bi][:, a:bnd],
                                                                               in1=EB[:, gi, a:bnd], op=ALU.mult),
                                       [b_pf[bi][a // 512], b_EB], [b_pb[bi][a // 512]],
                                       ("pool" if (a >= 512 and not isA) else "dve")))
                            a = bnd
                    return dl

                def stage2(idx):
                    it = items[idx]
                    gi, p, sbi, first, lastf = it
                    i = gi % 2
                    j = sbi % 2
                    bi = idx % 2
                    sl = slots_of(it)
                    Q = NS if p == "S" else 128
                    q0 = (NTP if p == "S" else p * 128) - sbs[sbi][1]

                    def f(e):
                        for g in range(G if Q != 128 else 1):
                            for k, (s, nk, kap, vap) in enumerate(sl):
                                if Q == 128:
                                    rhs = pb[bi][0:nk, s * SW:(s + 1) * SW]
                                    o1 = acc[bi][0:DH, 0:SW]
                                    o2 = den[bi][0:DH, 0:SW]
                                else:
                                    c = s * SW + g * 128
                                    rhs = pb[bi][0:nk, c:c + Q]
                                    o1 = acc[bi][0:DH, g * 128:g * 128 + Q]
                                    o2 = den[bi][0:DH, g * 128:g * 128 + Q]
                                e.matmul(o1, vap, rhs, start=(k == 0), stop=(k == len(sl) - 1))
                                r = e.matmul(o2, ones_b[0:nk, 0:DH], rhs, start=(k == 0),
                                             stop=(k == len(sl) - 1) and isA)
                            if not isA:
                                if Q == 128:
                                    o2 = den[bi][0:DH, 0:SW]
                                    eh, el = es_hi[0:1, gi, :], es_lo[0:1, gi, :]
                                else:
                                    o2 = den[bi][0:DH, g * 128:g * 128 + Q]
                                    eh, el = es_hi[0:1, gi, g * 128:g * 128 + Q], es_lo[0:1, gi, g * 128:g * 128 + Q]
                                e.matmul(o2, ones_b[0:1, 0:DH], eh, start=False, stop=False)
                                r = e.matmul(o2, ones_b[0:1, 0:DH], el, start=False, stop=True)
                        return r
                    pg.op("pe", f, reads=[b_pb[bi][0], b_pb[bi][1], b_vv[i], b_vc, c_ones] + ([] if isA else [b_eshl]),
                          writes=[b_acc[bi], b_den[bi]])

                    def vw(t):
                        return t[0:DH, 0:SW].rearrange("p (g q) -> p g q", g=G)[:, :, 0:Q]
                    dl = []
                    if isA:
                        dl.append((lambda e: e.reciprocal(out=vw(rd[bi]), in_=vw(den[bi])),
                                   [b_den[bi]], [b_rd[bi]], "dve"))
                    else:
                        dl.append((lambda e: e.reciprocal(out=vw(rd[bi]), in_=vw(den[bi])),
                                   [b_den[bi]], [b_rd[bi]], "dve"))
                    dl.append((lambda e: e.tensor_tensor(out=ost[j][0:DH, :, q0:q0 + Q], in0=vw(acc[bi]),
                                                         in1=vw(rd[bi]), op=ALU.mult),
                               [b_acc[bi], b_rd[bi]], [b_ost[j]], "dve"))
                    return dl

                def emit_dve(lst):
                    for (fn, rd_, wr_, en_) in lst:
                        pg.op(en_, fn, reads=rd_, writes=wr_)

                load_kv(0)
                load_q(0)
                emit_dve(stage1(0))
                prev_d2, prev_store = [], None
                for idx in range(len(items)):
                    gi, p = items[idx][0], items[idx][1]
                    if p == 0 and gi + 1 < NG:
                        load_kv(gi + 1)
                    d1 = stage1(idx + 1) if idx + 1 < len(items) else []
                    d2 = stage2(idx)
                    mix = []
                    while d1 or prev_d2:
                        if prev_d2:
                            mix.append(prev_d2.pop(0))
                        if d1:
                            mix.append(d1.pop(0))
                    emit_dve(mix)
                    if prev_store is not None:
                        store_o(prev_store)
                        prev_store = None
                    prev_d2 = d2
                    if items[idx][4]:
                        prev_store = items[idx][2]
                emit_dve(prev_d2)
                if prev_store is not None:
                    store_o(prev_store)
                pg.end_phase([miscsem, gsem])

            with contextlib.ExitStack() as ph:
                x1 = ph.enter_context(sbt("x1", [128, KC, 512], F32))
                aT = [ph.enter_context(sbt("aT%d" % i, [128, KC, 512], BF16)) for i in range(1)]
                uT = ph.enter_context(sbt("uT", [128, 64, 512], BF16))
                wt = [ph.enter_context(sbt("wtc%d" % i, [128, KC * 512], BF16)) for i in range(3)]
                sq = [ph.enter_context(sbt("sqc%d" % i, [128, 512], BF16)) for i in range(2)]
                rs = ph.enter_context(sbt("rsc", [128, 512], F32))
                x0c = [ph.enter_context(sbt("x0c%d" % i, [128, 512], F32)) for i in range(3)]
                rl = [ph.enter_context(sbt("rl%d" % i, [128, 512], F32)) for i in range(3)]
                xo = [ph.enter_context(sbt("xo%d" % i, [128, 512], F32)) for i in range(3)]
                ps = ph.enter_context(pst("psC", [128, 4096], F32))
                b_x1, b_aT, b_uT, b_wt = pg.buf("x1"), pg.bufs("aT", 1), pg.buf("uT"), pg.bufs("wtc", 3)
                b_sq, b_rs, b_x0c, b_rl, b_xo = pg.bufs("sq", 2), pg.buf("rs"), pg.bufs("x0c", 3), pg.bufs("rl", 3), pg.bufs("xo", 3)
                b_ps = pg.bufs("ps", 8)
                wblocks = []
                v16 = lambda t: t[:]
                for _blk in blocks:
                    for b in range(4):
                        wblocks.append((wo_s[par][b].rearrange("p k n -> p (k n)"), v16, castsems[("o", par)]))
                    for b in range(16):
                        wblocks.append((wu_s[par][b].rearrange("p k n -> p (k n)"), v16, castsems[("u", par)]))
                    for b in range(16):
                        wblocks.append((wd_s[par][b].rearrange("p k n -> p (k n)"), v16, castsems[("d", par)]))
                flush_casts(layer, "qoud")
                pg.hook_every = 3
                ws = WStream(pg, wt, b_wt, wblocks, None)
                cnt = dict(ps=1, x0=0, rl=0, xo=0)

                def nextps():
                    bi = 1 + (cnt["ps"] - 1) % 7
                    cnt["ps"] += 1
                    return bi

                for bidx, (t0, n) in enumerate(blocks):
                    ai = 0
                    pg.dma("sp", aT[ai][:, :, 0:n], oT[:, t0:t0 + n].rearrange("(c p) t -> p c t", p=128),
                           dst=b_aT[ai])
                    for b in range(4):
                        W, bW = ws.get()
                        Wv = W[:].rearrange("p (k n) -> p k n", k=KC)
                        for j in range(4):
                            m = 4 * b + j
                            bi = nextps()

                            def f(e, j=j, bi=bi, Wv=Wv, ai=ai, n=n):
                                for kc in range(KC):
                                    r = e.matmul(ps[:, bi * 512: bi * 512 + n], Wv[:, kc, j * 128:(j + 1) * 128],
                                                 aT[ai][:, kc, 0:n], start=(kc == 0), stop=(kc == KC - 1))
                                return r
                            pg.op("pe", f, reads=[bW, b_aT[ai]], writes=[b_ps[bi]])
                            xi = cnt["x0"] % 3
                            cnt["x0"] += 1
                            pg.dma("sp", x0c[xi][:, 0:n], xs[:, m, t0:t0 + n], dst=b_x0c[xi])
                            pg.op("dve", lambda e, m=m, bi=bi, xi=xi, n=n: e.tensor_tensor(
                                out=x1[:, m, 0:n], in0=ps[:, bi * 512: bi * 512 + n], in1=x0c[xi][:, 0:n], op=ALU.add),
                                reads=[b_ps[bi], b_x0c[xi]], writes=[b_x1])
                        ws.prefetch()
                    rmsnorm(ph, x1, b_x1, n, aT[ai], b_aT[ai], g_ffn, sq, b_sq, ps[:, 0:512], b_ps[0], rs, b_rs)
                    for b in range(16):
                        W, bW = ws.get()
                        Wv = W[:].rearrange("p (k n) -> p k n", k=KC)
                        for j in range(4):
                            fc = 4 * b + j
                            bi = nextps()

                            def f(e, j=j, bi=bi, Wv=Wv, ai=ai, n=n):
                                for kc in range(KC):
                                    r = e.matmul(ps[:, bi * 512: bi * 512 + n], Wv[:, kc, j * 128:(j + 1) * 128],
                                                 aT[ai][:, kc, 0:n], start=(kc == 0), stop=(kc == KC - 1))
                                return r
                            pg.op("pe", f, reads=[bW, b_aT[ai]], writes=[b_ps[bi]])
                            ri = cnt["rl"] % 3
                            cnt["rl"] += 1
                            pg.op("act", lambda e, ri=ri, bi=bi, n=n: e.activation(
                                out=rl[ri][:, 0:n], in_=ps[:, bi * 512: bi * 512 + n], func=AF.Relu),
                                reads=[b_ps[bi]], writes=[b_rl[ri]])
                            eng = "pool" if (fc % 2 == 0) else "dve"
                            pg.op(eng, lambda e, ri=ri, fc=fc, n=n: e.tensor_tensor(
                                out=uT[:, fc, 0:n], in0=rl[ri][:, 0:n], in1=rl[ri][:, 0:n], op=ALU.mult),
                                reads=[b_rl[ri]], writes=[b_uT])
                        ws.prefetch()
                    for m in range(16):
                        W, bW = ws.get()
                        Wv = W[:].rearrange("p (k n) -> p k n", k=64)
                        bi = nextps()

                        def f(e, bi=bi, Wv=Wv, n=n):
                            for fc in range(64):
                                r = e.matmul(ps[:, bi * 512: bi * 512 + n], Wv[:, fc, :], uT[:, fc, 0:n],
                                             start=(fc == 0), stop=(fc == 63))
                            return r
                        pg.op("pe", f, reads=[bW, b_uT], writes=[b_ps[bi]])
                        oi = cnt["xo"] % 3
                        cnt["xo"] += 1
                        pg.op("dve", lambda e, m=m, bi=bi, oi=oi, n=n: e.tensor_tensor(
                            out=xo[oi][:, 0:n], in0=ps[:, bi * 512: bi * 512 + n], in1=x1[:, m, 0:n], op=ALU.add),
                            reads=[b_ps[bi], b_x1], writes=[b_xo[oi]])
                        pg.dma("pool", xs[:, m, t0:t0 + n], xo[oi][:, 0:n], src=b_xo[oi])
                        ws.prefetch()
                pg.end_phase([miscsem, gsem])

        with contextlib.ExitStack() as ph:
            xf = [ph.enter_context(sbt("xf%d" % i, [128, KC, 128], F32)) for i in range(2)]
            yt = [ph.enter_context(sbt("yt%d" % i, [128, D], F32)) for i in range(2)]
            junk = ph.enter_context(sbt("junk", [128, D], F32))
            gfull = ph.enter_context(sbt("gfull", [128, D], F32))
            ssq = [ph.enter_context(sbt("ssq%d" % i, [128, 8], F32)) for i in range(2)]
            ps = ph.enter_context(pst("psE", [128, 4096], F32))
            b_xf, b_yt, b_junk, b_g, b_ssq = pg.bufs("xf", 2), pg.bufs("yt", 2), pg.buf("junk"), pg.buf("gfull"), pg.bufs("ssq", 2)
            b_ps = pg.bufs("psh", 2)
            pg.dma("sp", gfull[:], bass.AP(gfin, 0, [[0, 128], [1, D]]), dst=b_g)
            for tt in range(MT):
                rows = 128 if tt < MT - 1 else NS
                i = tt % 2
                pg.dma("sp", xf[i][:, :, 0:rows], xs[:, :, tt * 128: tt * 128 + rows], dst=b_xf[i])
                psv = ps[:, i * 2048:(i + 1) * 2048]

                def f(e, i=i, rows=rows, psv=psv):
                    for c in range(KC):
                        r = e.transpose(psv[0:rows, c * 128:(c + 1) * 128], xf[i][:, c, 0:rows], ident[:])
                    return r
                pg.op("pe", f, reads=[b_xf[i], c_ident], writes=[b_ps[i]])

                pg.op("act", lambda e, i=i: e.memzero(ssq[i][:]), writes=[b_ssq[i]])
                for k in range(4):
                    pg.op("act", lambda e, i=i, k=k, rows=rows, psv=psv: e.activation(
                        out=junk[0:rows, k * 512:(k + 1) * 512], in_=psv[0:rows, k * 512:(k + 1) * 512],
                        func=AF.Square, accum_out=ssq[i][0:rows, k:k + 1]),
                        reads=[b_ps[i], b_ssq[i]], writes=[b_junk, b_ssq[i]])
                pg.op("dve", lambda e, i=i, rows=rows: e.tensor_reduce(
                    out=ssq[i][0:rows, 4:5], in_=ssq[i][0:rows, 0:4], axis=mybir.AxisListType.X, op=ALU.add),
                    reads=[b_ssq[i]], writes=[b_ssq[i]])
                pg.op("dve", lambda e, i=i, rows=rows: e.tensor_scalar(
                    out=ssq[i][0:rows, 5:6], in0=ssq[i][0:rows, 4:5], scalar1=1.0 / D, scalar2=EPS,
                    op0=ALU.mult, op1=ALU.add), reads=[b_ssq[i]], writes=[b_ssq[i]])
                pg.op("act", lambda e, i=i, rows=rows: e.sqrt(out=ssq[i][0:rows, 6:7], in_=ssq[i][0:rows, 5:6]),
                      reads=[b_ssq[i]], writes=[b_ssq[i]])
                pg.op("dve", lambda e, i=i, rows=rows: e.reciprocal(out=ssq[i][0:rows, 7:8], in_=ssq[i][0:rows, 6:7]),
                      reads=[b_ssq[i]], writes=[b_ssq[i]])

                def fy(e, i=i, rows=rows, psv=psv):
                    for k in range(4):
                        r = e.scalar_tensor_tensor(out=yt[i][0:rows, k * 512:(k + 1) * 512],
                                                   in0=psv[0:rows, k * 512:(k + 1) * 512],
                                                   scalar=ssq[i][0:rows, 7:8], in1=gfull[0:rows, k * 512:(k + 1) * 512],
                                                   op0=ALU.mult, op1=ALU.mult)
                    return r
                pg.op("dve", fy, reads=[b_ps[i], b_ssq[i], b_g], writes=[b_yt[i]])
                dst = y_p[tt * 128:(tt + 1) * 128, :] if tt < MT - 1 else y_s
                pg.dma("pool", dst, yt[i][0:rows, :], src=b_yt[i])
            pg.end_phase([miscsem, gsem])
    return nc


def kernel(x_prompt, x_sample, cache_a_k, cache_a_v, cache_b_k, cache_b_v,
           norm_mix, norm_ffn, norm_final, a_w_qkv, a_w_o, a_rel_bias,
           b_w_qkv, b_w_o, b_sinks, w_up, w_down, _cfg=None):
    cfg = dict(CFG)
    if _cfg:
        cfg.update(_cfg)
    DEPTH, NTP, SEQ, SPLIT = cfg["DEPTH"], cfg["NTP"], cfg["SEQ"], cfg["SPLIT"]
    NA = (DEPTH + 1) // 2
    NB = DEPTH // 2
    f32 = np.float32
    A = lambda a: np.ascontiguousarray(np.asarray(a, dtype=f32))
    x_prompt, x_sample = A(x_prompt), A(x_sample)
    BATCH = x_prompt.shape[0]
    ncores = 8
    gains = np.concatenate([A(norm_mix)[:DEPTH].reshape(DEPTH, KC, 128), A(norm_ffn)[:DEPTH].reshape(DEPTH, KC, 128)],
                           axis=0)
    gains = np.ascontiguousarray(gains.transpose(2, 0, 1).reshape(128, 2 * DEPTH * KC))
    nidx = np.clip(127 - np.arange(768), -128, 128) + 128
    relext = np.ascontiguousarray(A(a_rel_bias)[:NA][:, :, nidx].reshape(NA, A_H * 768))
    nbm = max(NB, 1)
    common = dict(
        gains=gains, gfin=A(norm_final), relext=relext, sinks=A(b_sinks)[:nbm],
        w_aqkv=A(a_w_qkv)[:NA], w_ao=A(a_w_o)[:NA], w_bqkv=A(b_w_qkv)[:nbm], w_bo=A(b_w_o)[:nbm],
        w_up=A(w_up)[:DEPTH], w_dn=A(w_down)[:DEPTH])
    cache_a_k, cache_a_v, cache_b_k, cache_b_v = A(cache_a_k), A(cache_a_v), A(cache_b_k), A(cache_b_v)
    if NB == 0:
        cache_b_k = cache_b_v = np.zeros((1, ncores, B_CL, B_HKV, B_DH), f32)
        common.update(sinks=np.zeros((1, B_HQ), f32), w_bqkv=np.zeros((1, D, 3072), f32), w_bo=np.zeros((1, D, D), f32))
    in_maps = []
    for i in range(ncores):
        if SPLIT:
            b, half = i // 2, i % 2
            st = 0 if half == 0 else SEQ - NTP
        else:
            b, st = i, 0
        m = dict(common)
        m["x_p"] = np.ascontiguousarray(x_prompt[b, st:st + NTP])
        m["x_s"] = np.ascontiguousarray(x_sample[i])
        m["ca_k"] = np.ascontiguousarray(cache_a_k[:NA, i].reshape(NA, A_CL, D))
        m["ca_v"] = np.ascontiguousarray(cache_a_v[:NA, i].reshape(NA, A_CL, D))
        m["cb_k"] = np.ascontiguousarray(cache_b_k[:nbm, i].reshape(nbm, B_CL, 512))
        m["cb_v"] = np.ascontiguousarray(cache_b_v[:nbm, i].reshape(nbm, B_CL, 512))
        in_maps.append(m)
    nc = build_program(cfg)
    res = run_bass_kernel_spmd(nc, in_maps, core_ids=list(range(ncores)))
    R = res.results
    y_prompt = np.empty((BATCH, SEQ, D), f32)
    sakp = np.empty((NA, BATCH, A_CL, A_H, A_DH), f32)
    savp = np.empty_like(sakp)
    sbkp = np.empty((NB, BATCH, B_CL, B_HKV, B_DH), f32)
    sbvp = np.empty_like(sbkp)
    for i in range(ncores):
        if SPLIT:
            b, half = i // 2, i % 2
            if half == 0:
                y_prompt[b, 0:NTP] = R[i]["y_p"]
                continue
            y_prompt[b, NTP:SEQ] = R[i]["y_p"][2 * NTP - SEQ:]
        else:
            b = i
            y_prompt[b] = R[i]["y_p"]
        sakp[:, b] = R[i]["sa_kp"].reshape(NA, A_CL, A_H, A_DH)
        savp[:, b] = R[i]["sa_vp"].reshape(NA, A_CL, A_H, A_DH)
        if NB:
            sbkp[:, b] = R[i]["sb_kp"][:NB].reshape(NB, B_CL, B_HKV, B_DH)
            sbvp[:, b] = R[i]["sb_vp"][:NB].reshape(NB, B_CL, B_HKV, B_DH)
    y_sample = np.stack([R[i]["y_s"] for i in range(ncores)]).astype(f32)
    saks = np.stack([R[i]["sa_ks"].reshape(NA, A_CL, A_H, A_DH) for i in range(ncores)], axis=1)
    savs = np.stack([R[i]["sa_vs"].reshape(NA, A_CL, A_H, A_DH) for i in range(ncores)], axis=1)
    sbks = np.stack([R[i]["sb_ks"][:NB].reshape(NB, B_CL, B_HKV, B_DH) for i in range(ncores)], axis=1)
    sbvs = np.stack([R[i]["sb_vs"][:NB].reshape(NB, B_CL, B_HKV, B_DH) for i in range(ncores)], axis=1)
    return (y_prompt, y_sample, sakp, savp, sbkp, sbvp,
            np.ascontiguousarray(saks), np.ascontiguousarray(savs),
            np.ascontiguousarray(sbks), np.ascontiguousarray(sbvs))
```

```python
import contextlib
import numpy as np
import concourse.bass as bass
import concourse.mybir as mybir
from concourse.bass_utils import run_bass_kernel_spmd

F32 = mybir.dt.float32
BF16 = mybir.dt.bfloat16
AF = mybir.ActivationFunctionType
ALU = mybir.AluOpType

D = 2048
KC = 16
DFF = 8192
CHUNK = 64
NS = 16
EPS = 1e-6
A_H, A_DH, A_CL = 16, 128, 512
B_HQ, B_HKV, B_G, B_DH, B_CL = 32, 8, 4, 64, 128

CFG = dict(DEPTH=4, NTP=4736, SEQ=8192, SPLIT=True, T=512)


class Sem:
    def __init__(self, h):
        self.h = h
        self.count = 0


class Eng:
    def __init__(self, name, sem):
        self.name = name
        self.sem = sem
        self.waited = {}
        self.ops = []


class Buf:
    __slots__ = ("name", "w", "rs", "dsem")

    def __init__(self, name):
        self.name = name
        self.w = None
        self.rs = {}
        self.dsem = None


class _Stop(Exception):
    pass


class Prog:
    def __init__(self, nc, es):
        self.nph = 0
        self.max_ph = None
        self.hook = None
        self.hook_every = 6
        self._npool = 0
        self.nc = nc
        self.es = es
        self.engs = {}
        for nm in ("pe", "act", "dve", "pool", "sp"):
            s = Sem(es.enter_context(nc.semaphore("e_" + nm)))
            self.engs[nm] = Eng(nm, s)
        self.all_dsems = {"sp": [Sem(es.enter_context(nc.semaphore("dh%d" % i))) for i in range(28)],
                          "pool": [Sem(es.enter_context(nc.semaphore("ds%d" % i))) for i in range(20)]}
        self.free_dsems = {k: list(v) for k, v in self.all_dsems.items()}
        self.phase_bufs = []

    def buf(self, name):
        b = Buf(name)
        self.phase_bufs.append(b)
        return b

    def bufs(self, name, n):
        return [self.buf("%s%d" % (name, i)) for i in range(n)]

    def _need(self, E, ev):
        sem, val = ev
        if E.waited.get(id(sem), 0) >= val:
            return
        E.waited[id(sem)] = val
        E.ops.append(lambda e, s=sem.h, v=val: e.wait_ge(s, v))

    def op(self, en, fn, reads=(), writes=()):
        E = self.engs[en]
        for b in reads:
            if b.w is not None:
                self._need(E, b.w)
        for b in writes:
            if b.w is not None and b.w[0] is not E.sem:
                self._need(E, b.w)
            for ev in b.rs.values():
                if ev[0] is not E.sem:
                    self._need(E, ev)
        E.sem.count += 1
        ev = (E.sem, E.sem.count)
        E.ops.append(lambda e, f=fn, s=E.sem.h: f(e).then_inc(s, 1))
        for b in writes:
            b.w = ev
            b.rs = {}
        for b in reads:
            b.rs[id(E.sem)] = ev
        return ev

    def _dsem(self, b, qn):
        if b.dsem is None:
            b.dsem = (qn, self.free_dsems[qn].pop())
        assert b.dsem[0] == qn, "buffer %s is DMA'd from two queue kinds" % b.name
        return b.dsem[1]

    def dma(self, qn, out_ap, in_ap, src=None, dst=None, sem=None):
        Q = self.engs[qn]
        if src is not None and src.w is not None:
            self._need(Q, src.w)
        if dst is not None:
            if dst.w is not None:
                self._need(Q, dst.w)
            for ev in dst.rs.values():
                self._need(Q, ev)
        S = sem if sem is not None else self._dsem(dst if dst is not None else src, qn)
        S.count += 16
        ev = (S, S.count)
        Q.ops.append(lambda e, o=out_ap, i=in_ap, s=S.h: e.dma_start(out=o, in_=i).then_inc(s, 16))
        if dst is not None:
            dst.w = ev
            dst.rs = {}
        if src is not None:
            src.rs[id(S)] = ev
        if qn == "pool" and sem is None and self.hook is not None:
            self._npool += 1
            if self._npool % self.hook_every == 0:
                self.hook()
        return ev

    def wait_sem(self, en, S):
        self._need(self.engs[en], (S, S.count))

    def end_phase(self, extra_sems=()):
        sems = [E.sem for E in self.engs.values() if E.sem.count > 0]
        sems += [s for v in self.all_dsems.values() for s in v if s.count > 0] + [s for s in extra_sems if s.count > 0]
        for E in self.engs.values():
            for s in sems:
                if s is not E.sem:
                    self._need(E, (s, s.count))
        with self.nc.Block() as block:
            def mk(E):
                def body(e):
                    for f in E.ops:
                        f(e)
                return body
            block.tensor(mk(self.engs["pe"]))
            block.scalar(mk(self.engs["act"]))
            block.vector(mk(self.engs["dve"]))
            block.gpsimd(mk(self.engs["pool"]))
            block.sync(mk(self.engs["sp"]))
        for E in self.engs.values():
            E.ops = []
        for b in self.phase_bufs:
            b.dsem = None
        self.phase_bufs = []
        self.free_dsems = {k: list(v) for k, v in self.all_dsems.items()}
        self.nph += 1
        if self.max_ph is not None and self.nph >= self.max_ph:
            raise _Stop()


class WStream:
    def __init__(self, pg, tiles, bufs, blocks, presems):
        self.pg, self.tiles, self.bufs, self.blocks, self.presems = pg, tiles, bufs, blocks, presems
        self.nload = 0
        self.nget = 0

    def _load(self):
        k = self.nload
        if k >= len(self.blocks):
            return
        ap, shape_sel, psem = self.blocks[k]
        i = k % len(self.bufs)
        if psem is not None:
            self.pg.wait_sem("sp", psem)
        self.pg.dma("sp", shape_sel(self.tiles[i]), ap, dst=self.bufs[i])
        self.nload += 1

    def get(self):
        k = self.nget
        while self.nload < min(k + len(self.bufs), len(self.blocks)):
            self._load()
        self.nget += 1
        i = k % len(self.bufs)
        return self.tiles[i], self.bufs[i]

    def prefetch(self):
        while self.nload < min(self.nget + len(self.bufs), len(self.blocks)):
            self._load()


def build_program(cfg):
    DEPTH, NTP, T = cfg["DEPTH"], cfg["NTP"], cfg["T"]
    NT = NTP + NS
    MT = NTP // 128 + 1
    NA = (DEPTH + 1) // 2
    NB = DEPTH // 2
    assert NTP % 128 == 0

    nc = bass.Bass("TRN2", target_bir_lowering=False)

    uid = [0]

    def sbt(name, shape, dt):
        uid[0] += 1
        return nc.sbuf_tensor("%s_%d" % (name, uid[0]), shape, dt)

    def pst(name, shape, dt):
        uid[0] += 1
        return nc.psum_tensor("%s_%d" % (name, uid[0]), shape, dt)

    def din(name, shape, dt=F32):
        return nc.dram_tensor(name, list(shape), dt, kind="ExternalInput")

    def dout(name, shape):
        return nc.dram_tensor(name, list(shape), F32, kind="ExternalOutput")

    def dscr(name, shape, dt):
        return nc.dram_tensor(name, list(shape), dt, kind="Internal")

    x_p = din("x_p", [NTP, D]).ap()
    x_s = din("x_s", [NS, D]).ap()
    ca_k = din("ca_k", [NA, A_CL, D]).ap()
    ca_v = din("ca_v", [NA, A_CL, D]).ap()
    cb_k = din("cb_k", [max(NB, 1), B_CL, 512]).ap()
    cb_v = din("cb_v", [max(NB, 1), B_CL, 512]).ap()
    gains = din("gains", [128, 2 * DEPTH * KC]).ap()
    gfin = din("gfin", [D])
    relext = din("relext", [NA, A_H * 768])
    sinks = din("sinks", [max(NB, 1), B_HQ])
    w_aqkv = din("w_aqkv", [NA, D, 3 * D]).ap()
    w_ao = din("w_ao", [NA, D, D]).ap()
    w_bqkv = din("w_bqkv", [max(NB, 1), D, 3072]).ap()
    w_bo = din("w_bo", [max(NB, 1), D, D]).ap()
    w_up = din("w_up", [DEPTH, D, DFF]).ap()
    w_dn = din("w_dn", [DEPTH, DFF, D]).ap()

    y_p = dout("y_p", [NTP, D]).ap()
    y_s = dout("y_s", [NS, D]).ap()
    sa_kp = dout("sa_kp", [NA, A_CL, D]).ap()
    sa_vp = dout("sa_vp", [NA, A_CL, D]).ap()
    sb_kp = dout("sb_kp", [max(NB, 1), B_CL, 512]).ap()
    sb_vp = dout("sb_vp", [max(NB, 1), B_CL, 512]).ap()
    sa_ks = dout("sa_ks", [NA, A_CL, D]).ap()
    sa_vs = dout("sa_vs", [NA, A_CL, D]).ap()
    sb_ks = dout("sb_ks", [max(NB, 1), B_CL, 512]).ap()
    sb_vs = dout("sb_vs", [max(NB, 1), B_CL, 512]).ap()

    xs = dscr("xs", [128, KC, NT], F32).ap()
    qT = dscr("qT", [D, NT], BF16).ap()
    kT = dscr("kT", [D, NT], BF16).ap()
    oT = dscr("oT", [D, NT], BF16).ap()
    v_a = dscr("v_a", [A_H, 128, MT, A_DH], BF16).ap()
    v_b = dscr("v_b", [B_HKV, 128, MT, B_DH], BF16).ap()
    zt = [dscr("zt%d" % i, [128, A_H * 768], F32) for i in range(NA)]
    wq_s = [dscr("wq_s%d" % i, [12, 128, KC, 512], BF16).ap() for i in range(2)]
    wo_s = [dscr("wo_s%d" % i, [4, 128, KC, 512], BF16).ap() for i in range(2)]
    wu_s = [dscr("wu_s%d" % i, [16, 128, KC, 512], BF16).ap() for i in range(2)]
    wd_s = [dscr("wd_s%d" % i, [16, 128, 64, 128], BF16).ap() for i in range(2)]

    blocks = []
    t0 = 0
    while t0 < NTP:
        n = min(T, NTP - t0)
        blocks.append([t0, n])
        t0 += n
    if blocks[-1][1] + NS <= 512 and blocks[-1][1] <= 256:
        blocks[-1][1] += NS
    else:
        blocks.append([NTP, NS])

    try:
        _build_body(nc, cfg, locals())
    except _Stop:
        pass
    return nc


def _build_body(nc, cfg, L):
    globals_needed = L
    DEPTH, NTP, T = cfg["DEPTH"], cfg["NTP"], cfg["T"]
    NT, MT, NA, NB = L["NT"], L["MT"], L["NA"], L["NB"]
    sbt, pst, blocks = L["sbt"], L["pst"], L["blocks"]
    (x_p, x_s, ca_k, ca_v, cb_k, cb_v, gains, gfin, relext, sinks, w_aqkv, w_ao, w_bqkv, w_bo, w_up, w_dn) = [
        L[k] for k in "x_p x_s ca_k ca_v cb_k cb_v gains gfin relext sinks w_aqkv w_ao w_bqkv w_bo w_up w_dn".split()]
    (y_p, y_s, sa_kp, sa_vp, sb_kp, sb_vp, sa_ks, sa_vs, sb_ks, sb_vs) = [
        L[k] for k in "y_p y_s sa_kp sa_vp sb_kp sb_vp sa_ks sa_vs sb_ks sb_vs".split()]
    (xs, qT, kT, oT, v_a, v_b, zt, wq_s, wo_s, wu_s, wd_s) = [
        L[k] for k in "xs qT kT oT v_a v_b zt wq_s wo_s wu_s wd_s".split()]
    with contextlib.ExitStack() as es:
        pg = Prog(nc, es)
        pg.max_ph = cfg.get("NPH")
        SKIP = cfg.get("SKIP", ())
        castsems = {(k, par): Sem(es.enter_context(nc.semaphore("c_%s%d" % (k, par))))
                    for k in ("q", "o", "u", "d") for par in range(2)}
        miscsem = Sem(es.enter_context(nc.semaphore("misc")))
        gsem = Sem(es.enter_context(nc.semaphore("gsem")))

        ident = es.enter_context(sbt("ident", [128, 128], F32))
        ones_m = es.enter_context(sbt("ones_m", [128, 128], BF16))
        ones_b = es.enter_context(sbt("ones_b", [128, 128], BF16))
        gn = es.enter_context(sbt("gn", [128, 2 * DEPTH * KC], F32))
        c_ident, c_ones, c_gn = Buf("ident"), Buf("ones"), Buf("gn")

        def emit_consts():
            pg.op("pool", lambda e: e.memset(ident[:], 0.0), writes=[c_ident])
            pg.op("pool", lambda e: e.affine_select(out=ident[:], in_=ident[:], pattern=[[-1, 128]],
                                                    compare_op=ALU.not_equal, fill=1.0, base=0,
                                                    channel_multiplier=1), reads=[c_ident], writes=[c_ident])

            def f2(e):
                e.memset(ones_m[:], 1.0 / D)
                return e.memset(ones_b[:], 1.0)
            pg.op("pool", f2, writes=[c_ones])
            pg.dma("sp", gn[:], gains, dst=c_gn, sem=gsem)

        pending = []

        def emit_casts(layer):
            if "cast" in SKIP:
                return
            par = layer % 2
            slot = layer // 2
            isA = (layer % 2 == 0)
            wqkv = (w_aqkv if isA else w_bqkv)[slot]
            wo = (w_ao if isA else w_bo)[slot]
            nq = 12 if isA else 6
            for b in range(nq):
                src = wqkv[:, b * 512:(b + 1) * 512].rearrange("(kc p) n -> p kc n", p=128)
                pending.append((wq_s[par][b], src, castsems[("q", par)], layer, "q"))
            for b in range(4):
                src = wo[:, b * 512:(b + 1) * 512].rearrange("(kc p) n -> p kc n", p=128)
                pending.append((wo_s[par][b], src, castsems[("o", par)], layer, "o"))
            for b in range(16):
                src = w_up[layer][:, b * 512:(b + 1) * 512].rearrange("(kc p) n -> p kc n", p=128)
                pending.append((wu_s[par][b], src, castsems[("u", par)], layer, "u"))
            for b in range(16):
                src = w_dn[layer][:, b * 128:(b + 1) * 128].rearrange("(kc p) n -> p kc n", p=128)
                pending.append((wd_s[par][b], src, castsems[("d", par)], layer, "d"))

        def drip(n=1):
            for _ in range(n):
                if not pending:
                    return
                dst_, src_, sem_, _l, _k = pending.pop(0)
                pg.dma("pool", dst_, src_, sem=sem_)
        pg.hook = drip

        def flush_casts(layer, kinds):
            while pending and pending[0][3] <= layer and (pending[0][3] < layer or pending[0][4] in kinds):
                drip(1)

        with contextlib.ExitStack() as ph:
            xin = [ph.enter_context(sbt("xin%d" % i, [128, D], F32)) for i in range(2)]
            xst = [ph.enter_context(sbt("xst%d" % i, [128, KC, 128], F32)) for i in range(2)]
            extb = ph.enter_context(sbt("extb", [128, A_H * 768], F32))
            ps = ph.enter_context(pst("ps0", [128, 4096], F32))
            b_xin, b_xst, b_ps = pg.bufs("xin", 2), pg.bufs("xst", 2), pg.bufs("ps", 8)
            b_ext = pg.buf("extb")
            emit_consts()
            for s in range(NA):
                pg.dma("pool", sa_ks[s, 0:A_CL - NS, :], ca_k[s, NS:A_CL, :], sem=miscsem)
                pg.dma("pool", sa_vs[s, 0:A_CL - NS, :], ca_v[s, NS:A_CL, :], sem=miscsem)
            for s in range(NB):
                pg.dma("pool", sb_ks[s, 0:B_CL - NS, :], cb_k[s, NS:B_CL, :], sem=miscsem)
                pg.dma("pool", sb_vs[s, 0:B_CL - NS, :], cb_v[s, NS:B_CL, :], sem=miscsem)
            emit_casts(0)
            drip(12)
            pg.hook_every = 4
            for s in range(NA):
                src = bass.AP(relext, s * A_H * 768, [[0, 128], [1, A_H * 768]])
                pg.dma("sp", extb[:], src, dst=b_ext)
                pg.dma("sp", zt[s].ap(), extb[:], src=b_ext)
            pscnt = 0
            for tt in range(MT):
                rows = 128 if tt < MT - 1 else NS
                src = x_p[tt * 128:(tt + 1) * 128, :] if tt < MT - 1 else x_s
                i = tt % 2
                pg.dma("sp", xin[i][0:rows, :], src, dst=b_xin[i])
                for k in range(4):
                    bi = pscnt % 8
                    pscnt += 1

                    def f(e, i=i, k=k, bi=bi, rows=rows):
                        for j in range(4):
                            c = 4 * k + j
                            r = e.transpose(ps[:, bi * 512 + j * 128: bi * 512 + j * 128 + rows],
                                            xin[i][0:rows, c * 128:(c + 1) * 128], ident[0:rows, 0:rows])
                        return r
                    pg.op("pe", f, reads=[b_xin[i], c_ident], writes=[b_ps[bi]])
                    src_ps = ps[:, bi * 512:(bi + 1) * 512].rearrange("p (j t) -> p j t", j=4)[:, :, 0:rows]
                    dstv = xst[i][:, 4 * k:4 * k + 4, 0:rows]
                    if k % 2 == 0:
                        pg.op("act", lambda e, o=dstv, s_=src_ps: e.copy(out=o, in_=s_),
                              reads=[b_ps[bi]], writes=[b_xst[i]])
                    else:
                        pg.op("dve", lambda e, o=dstv, s_=src_ps: e.tensor_copy(out=o, in_=s_),
                              reads=[b_ps[bi]], writes=[b_xst[i]])
                pg.dma("pool", xs[:, :, tt * 128: tt * 128 + rows], xst[i][:, :, 0:rows], src=b_xst[i])
            pg.end_phase([miscsem, gsem])

        for layer in range(DEPTH):
            isA = (layer % 2 == 0)
            slot = layer // 2
            par = layer % 2
            nqb = 12 if isA else 6
            CL = A_CL if isA else B_CL
            g_mix = gn[:, layer * KC:(layer + 1) * KC]
            g_ffn = gn[:, (DEPTH + layer) * KC:(DEPTH + layer + 1) * KC]
            k_st_p = (sa_kp if isA else sb_kp)[slot]
            v_st_p = (sa_vp if isA else sb_vp)[slot]
            k_st_s = (sa_ks if isA else sb_ks)[slot]
            v_st_s = (sa_vs if isA else sb_vs)[slot]

            def rmsnorm(ph_t, xb_t, b_xb, n, hT_t, b_hT, g_ap, sq, b_sq, psr, b_psr, rs, b_rs):
                for c in range(KC):
                    i = c % 2
                    pg.op("act", lambda e, c=c, i=i: e.activation(out=sq[i][:, 0:n], in_=xb_t[:, c, 0:n],
                                                                  func=AF.Square),
                          reads=[b_xb], writes=[b_sq[i]])
                    pg.op("pe", lambda e, c=c, i=i: e.matmul(psr[:, 0:n], ones_m[:], sq[i][:, 0:n],
                                                             start=(c == 0), stop=(c == KC - 1)),
                          reads=[b_sq[i], c_ones], writes=[b_psr])
                pg.op("dve", lambda e: e.tensor_scalar(out=rs[:, 0:n], in0=psr[:, 0:n], scalar1=EPS, scalar2=None,
                                                       op0=ALU.add), reads=[b_psr], writes=[b_rs])
                pg.op("act", lambda e: e.sqrt(out=rs[:, 0:n], in_=rs[:, 0:n]), reads=[b_rs], writes=[b_rs])
                pg.op("dve", lambda e: e.reciprocal(out=rs[:, 0:n], in_=rs[:, 0:n]), reads=[b_rs], writes=[b_rs])
                for c in range(KC):
                    pg.op("dve", lambda e, c=c: e.scalar_tensor_tensor(
                        out=hT_t[:, c, 0:n], in0=xb_t[:, c, 0:n], scalar=g_ap[:, c:c + 1], in1=rs[:, 0:n],
                        op0=ALU.mult, op1=ALU.mult), reads=[b_xb, b_rs, c_gn], writes=[b_hT])

            with contextlib.ExitStack() as ph:
                xb = [ph.enter_context(sbt("xb%d" % i, [128, KC, 512], F32)) for i in range(2)]
                hT = [ph.enter_context(sbt("hT%d" % i, [128, KC, 512], BF16)) for i in range(2)]
                wt = [ph.enter_context(sbt("wt%d" % i, [128, KC, 512], BF16)) for i in range(3)]
                sq = [ph.enter_context(sbt("sq%d" % i, [128, 512], BF16)) for i in range(2)]
                rs = ph.enter_context(sbt("rs", [128, 512], F32))
                stg = [ph.enter_context(sbt("stg%d" % i, [128, 512], BF16)) for i in range(6)]
                vstg = [ph.enter_context(sbt("vstg%d" % i, [128, 512], BF16)) for i in range(3)]
                sstg = [ph.enter_context(sbt("sstg%d" % i, [128, 512], F32)) for i in range(3)]
                ps = ph.enter_context(pst("psA", [128, 4096], F32))
                b_xb, b_hT, b_wt, b_sq = pg.bufs("xb", 2), pg.bufs("hT", 2), pg.bufs("wt", 3), pg.bufs("sq", 2)
                b_rs, b_stg, b_vstg, b_sstg = pg.buf("rs"), pg.bufs("stg", 6), pg.bufs("vstg", 3), pg.bufs("sstg", 3)
                b_ps = pg.bufs("ps", 8)
                flush_casts(layer, "q")
                pg.hook_every = 16 if layer == 0 else 40
                if layer + 1 < DEPTH and 'nextcast' not in SKIP:
                    emit_casts(layer + 1)
                wblocks = []
                for _blk in blocks:
                    for b in range(nqb):
                        wblocks.append((wq_s[par][b], (lambda t: t[:]), castsems[("q", par)]))
                ws = WStream(pg, wt, b_wt, wblocks, None)
                cnt = dict(ps=1, stg=0, vstg=0, sstg=0)

                def nextps():
                    bi = 1 + (cnt["ps"] - 1) % 7
                    cnt["ps"] += 1
                    return bi

                def prep(bidx):
                    if bidx >= len(blocks):
                        return
                    t0_, n_ = blocks[bidx]
                    xi_ = bidx % 2
                    pg.dma("sp", xb[xi_][:, :, 0:n_], xs[:, :, t0_:t0_ + n_], dst=b_xb[xi_])
                    rmsnorm(ph, xb[xi_], b_xb[xi_], n_, hT[xi_], b_hT[xi_], g_mix, sq, b_sq,
                            ps[:, 0:512], b_ps[0], rs, b_rs)

                prep(0)
                for bidx, (t0, n) in enumerate(blocks):
                    xi = bidx % 2
                    tiles = []
                    c0 = 0
                    while c0 < n:
                        tok = t0 + c0
                        if tok >= NTP:
                            tiles.append((c0, NS, "s", CL - NS, tok))
                            c0 += NS
                        else:
                            r = min(128, NTP - tok, n - c0)
                            srow = tok - (NTP - CL) if tok >= NTP - CL else None
                            tiles.append((c0, r, "p", srow, tok))
                            c0 += r
                    for b in range(nqb):
                        W, bW = ws.get()
                        if isA:
                            kind = "q" if b < 4 else ("k" if b < 8 else "v")
                            cb = b % 4
                        else:
                            kind = "q" if b < 4 else ("k" if b == 4 else "v")
                            cb = b if b < 4 else 0
                        if kind in ("q", "k") and "qk" not in SKIP:
                            dstT = qT if kind == "q" else kT
                            for j in range(4):
                                bi = nextps()

                                def f(e, j=j, bi=bi, W=W, xi=xi, n=n):
                                    for kc in range(KC):
                                        r = e.matmul(ps[:, bi * 512: bi * 512 + n], W[:, kc, j * 128:(j + 1) * 128],
                                                     hT[xi][:, kc, 0:n], start=(kc == 0), stop=(kc == KC - 1))
                                    return r
                                pg.op("pe", f, reads=[bW, b_hT[xi]], writes=[b_ps[bi]])
                                si = cnt["stg"] % 6
                                cnt["stg"] += 1
                                pg.op("act", lambda e, si=si, bi=bi, n=n: e.copy(out=stg[si][:, 0:n],
                                                                                  in_=ps[:, bi * 512: bi * 512 + n]),
                                      reads=[b_ps[bi]], writes=[b_stg[si]])
                                r0 = cb * 512 + j * 128
                                pg.dma("pool", dstT[r0:r0 + 128, t0:t0 + n], stg[si][:, 0:n], src=b_stg[si])
                        if kind in ("k", "v") and "kv" not in SKIP:
                            for (c0, r, tk, srow, tok) in tiles:
                                if kind == "k" and srow is None:
                                    continue
                                bi = nextps()

                                def f(e, bi=bi, W=W, xi=xi, c0=c0, r=r):
                                    for kc in range(KC):
                                        rr = e.matmul(ps[0:r, bi * 512:(bi + 1) * 512], hT[xi][:, kc, c0:c0 + r],
                                                      W[:, kc, :], start=(kc == 0), stop=(kc == KC - 1))
                                    return rr
                                pg.op("pe", f, reads=[bW, b_hT[xi]], writes=[b_ps[bi]])
                                if kind == "v":
                                    vi = cnt["vstg"] % 3
                                    cnt["vstg"] += 1
                                    pg.op("dve", lambda e, vi=vi, bi=bi, r=r: e.tensor_copy(
                                        out=vstg[vi][0:r, :], in_=ps[0:r, bi * 512:(bi + 1) * 512]),
                                        reads=[b_ps[bi]], writes=[b_vstg[vi]])
                                    m = tok // 128
                                    if isA:
                                        dst = v_a[4 * cb:4 * cb + 4, 0:r, m, :].rearrange("h p d -> p h d")
                                        srcv = vstg[vi][0:r, :].rearrange("p (h d) -> p h d", h=4)
                                    else:
                                        dst = v_b[:, 0:r, m, :].rearrange("h p d -> p h d")
                                        srcv = vstg[vi][0:r, :].rearrange("p (h d) -> p h d", h=8)
                                    if "vstore" not in SKIP:
                                        pg.dma("pool", dst, srcv, src=b_vstg[vi])
                                if srow is not None:
                                    ssi = cnt["sstg"] % 3
                                    cnt["sstg"] += 1
                                    if kind == "v":
                                        pg.op("dve", lambda e, ssi=ssi, bi=bi, r=r: e.tensor_copy(
                                            out=sstg[ssi][0:r, :], in_=ps[0:r, bi * 512:(bi + 1) * 512]),
                                            reads=[b_ps[bi]], writes=[b_sstg[ssi]])
                                    else:
                                        pg.op("act", lambda e, ssi=ssi, bi=bi, r=r: e.copy(
                                            out=sstg[ssi][0:r, :], in_=ps[0:r, bi * 512:(bi + 1) * 512]),
                                            reads=[b_ps[bi]], writes=[b_sstg[ssi]])
                                    if tk == "p":
                                        dsto = (k_st_p if kind == "k" else v_st_p)
                                    else:
                                        dsto = (k_st_s if kind == "k" else v_st_s)
                                    if "sstore" not in SKIP:
                                        pg.dma("pool", dsto[srow:srow + r, cb * 512:(cb + 1) * 512],
                                               sstg[ssi][0:r, :], src=b_sstg[ssi])
                        ws.prefetch()
                        if b == 1:
                            prep(bidx + 1)
                pg.end_phase([miscsem, gsem])

            with contextlib.ExitStack() as ph:
                if isA:
                    NG, G, DH, NSL, H_PART = A_H, 1, 128, 5, 128
                else:
                    NG, G, DH, NSL, H_PART = B_HKV, 4, 64, 2, 64
                SW = G * 128
                NCOL = NSL * SW
                scale = float(DH) ** -0.5
                kq = [ph.enter_context(sbt("k_t%d" % i, [128, NT], BF16)) for i in range(2)]
                QB = NT if isA else (1024 + NS)
                qq = [ph.enter_context(sbt("q_t%d" % i, [128, G, QB], BF16)) for i in range(2)]
                vv = [ph.enter_context(sbt("v_t%d" % i, [128, MT, DH], BF16)) for i in range(2)]
                ost = [ph.enter_context(sbt("ost%d" % i, [128, G, QB], BF16)) for i in range(2)]
                EB = ph.enter_context(sbt("EB", [128, NG, NCOL], F32))
                pf = [ph.enter_context(sbt("pf%d" % i, [128, 1024], F32)) for i in range(2)]
                pb = [ph.enter_context(sbt("pb%d" % i, [128, 1024], BF16)) for i in range(2)]
                rd = [ph.enter_context(sbt("rd%d" % i, [128, 512], F32)) for i in range(2)]
                kcs = [ph.enter_context(sbt("kcs%d" % i, [128, D if isA else 512], F32)) for i in range(2)]
                kcT = ph.enter_context(sbt("kcT", [128, NG, CL], BF16))
                vc = ph.enter_context(sbt("vc", [128, CL // 128, D if isA else 512], BF16))
                ps = ph.enter_context(pst("psB", [128, 4096], F32))
                b_kq, b_qq, b_vv, b_ost = pg.bufs("kq", 2), pg.bufs("qq", 2), pg.bufs("vv", 2), pg.bufs("ost", 2)
                b_EB, b_rd = pg.buf("EB"), pg.bufs("rd", 2)
                b_pf = [pg.bufs("pf%d_" % i, 2) for i in range(2)]
                b_pb = [pg.bufs("pb%d_" % i, 2) for i in range(2)]
                b_kcs, b_kcT, b_vc = pg.bufs("kcs", 2), pg.buf("kcT"), pg.buf("vc")
                b_sps, b_acc, b_den = pg.bufs("sps", 2), pg.bufs("acc", 2), pg.bufs("den", 2)
                sps = [ps[:, 0:1024], ps[:, 1024:2048]]
                acc = [ps[:, 2048:2560], ps[:, 2560:3072]]
                den = [ps[:, 3072:3584], ps[:, 3584:4096]]
                cache_k = (ca_k if isA else cb_k)[slot]
                cache_v = (ca_v if isA else cb_v)[slot]
                FW = D if isA else 512

                if isA:
                    EBv = EB[:].rearrange("p h (s q) -> p h s q", s=NSL)
                    for s in range(NSL):
                        off = 639 - 128 * s
                        src = bass.AP(zt[slot], off, [[A_H * 768 - 1, 128], [768, A_H], [1, 128]])
                        pg.dma("sp", EBv[:, :, s, :], src, dst=b_EB)
                    for h0 in range(0, NG, 4):
                        pg.op("act", lambda e, h0=h0: e.activation(out=EB[:, h0:h0 + 4, :], in_=EB[:, h0:h0 + 4, :],
                                                                   func=AF.Exp), reads=[b_EB], writes=[b_EB])

                    def fm(e):
                        e.memset(EBv[0:64, :, 0, 64:128], 0.0)
                        return e.memset(EBv[64:128, :, NSL - 1, 0:64], 0.0)
                    pg.op("pool", fm, reads=[b_EB], writes=[b_EB])
                else:
                    esk = ph.enter_context(sbt("esk", [128, B_HQ], F32))
                    esb = ph.enter_context(sbt("esb", [64, B_HKV, 512], F32))
                    iot = ph.enter_context(sbt("iot", [128, 1024], F32))
                    b_esk, b_esb, b_iot = pg.buf("esk"), pg.buf("esb"), pg.buf("iot")
                    pg.dma("sp", esk[:], bass.AP(sinks, slot * B_HQ, [[0, 128], [1, B_HQ]]), dst=b_esk)
                    pg.op("act", lambda e: e.activation(out=esk[:], in_=esk[:], func=AF.Exp),
                          reads=[b_esk], writes=[b_esk])

                    pg.op("pool", lambda e: e.iota(iot[:], pattern=[[128, 2], [0, 4], [-1, 128]], base=-128,
                                                   channel_multiplier=1, allow_small_or_imprecise_dtypes=True),
                          writes=[b_iot])
                    pg.op("pool", lambda e: e.memset(esb[:], 0.0), writes=[b_esb])
                    pg.op("act", lambda e: e.activation(out=iot[:], in_=iot[:], func=AF.Abs),
                          reads=[b_iot], writes=[b_iot])
                    EBv = EB[:].rearrange("p h (s g q) -> p h s g q", s=2, g=4)
                    iov = iot[:].rearrange("p (s g q) -> p s g q", s=2, g=4)
                    for kvh in range(B_HKV):
                        for g in range(B_G):
                            hq = kvh * B_G + g
                            slope = 2.0 ** (-8.0 * (hq + 1) / B_HQ)
                            pg.op("act", lambda e, kvh=kvh, g=g, slope=slope: e.activation(
                                out=EBv[:, kvh, :, g, :], in_=iov[:, :, g, :], func=AF.Exp, scale=-slope),
                                reads=[b_iot], writes=[b_EB])
                            pg.op("dve", lambda e, kvh=kvh, g=g, hq=hq: e.tensor_scalar(
                                out=esb[:, kvh, g * 128:(g + 1) * 128], in0=esb[:, kvh, g * 128:(g + 1) * 128],
                                scalar1=esk[0:64, hq:hq + 1], scalar2=None, op0=ALU.add),
                                reads=[b_esk, b_esb], writes=[b_esb])

                    es_hi = ph.enter_context(sbt("es_hi", [1, B_HKV, 512], BF16))
                    es_lo = ph.enter_context(sbt("es_lo", [1, B_HKV, 512], BF16))
                    b_eshl = pg.buf("eshl")
                    pg.op("dve", lambda e: e.tensor_copy(out=es_hi[:], in_=esb[0:1, :, :]), reads=[b_esb], writes=[b_eshl])
                    pg.op("dve", lambda e: e.tensor_tensor(out=es_lo[:], in0=esb[0:1, :, :], in1=es_hi[:],
                                                           op=ALU.subtract), reads=[b_esb, b_eshl], writes=[b_eshl])

                    def fm(e):
                        for kvh in range(B_HKV):
                            e.memset(EBv[0:64, kvh, 0, :, 64:128], 0.0)
                            r = e.memset(EBv[64:128, kvh, 1, :, 0:64], 0.0)
                        return r
                    pg.op("pool", fm, reads=[b_EB], writes=[b_EB])

                pg.dma("pool", vc[:], cache_v.rearrange("(s p) f -> p s f", p=128), dst=b_vc)
                tcnt = 0
                for s in range(CL // 128):
                    ki = s % 2
                    pg.dma("sp", kcs[ki][:], cache_k[s * 128:(s + 1) * 128, :], dst=b_kcs[ki])
                    for h0 in range(0, NG, 4):
                        si = tcnt % 2
                        tcnt += 1

                        def f(e, ki=ki, h0=h0, si=si):
                            for j in range(4):
                                h = h0 + j
                                r = e.transpose(sps[si][0:DH, j * 128:(j + 1) * 128],
                                                kcs[ki][:, h * DH:(h + 1) * DH], ident[:])
                            return r
                        pg.op("pe", f, reads=[b_kcs[ki], c_ident], writes=[b_sps[si]])
                        pg.op("dve", lambda e, h0=h0, si=si, s=s: e.tensor_copy(
                            out=kcT[0:DH, h0:h0 + 4, s * 128:(s + 1) * 128],
                            in_=sps[si][0:DH, 0:512].rearrange("p (j k) -> p j k", j=4)),
                            reads=[b_sps[si]], writes=[b_kcT])

                npairs = NTP // 128
                PPB = npairs if isA else 8
                sbs = []
                items = []
                for gi in range(NG):
                    p0 = 0
                    while p0 < npairs:
                        p1 = min(npairs, p0 + PPB)
                        last = (p1 == npairs)
                        sbi = len(sbs)
                        sbs.append((gi, p0 * 128, (p1 - p0) * 128 + (NS if last else 0)))
                        for p in range(p0, p1):
                            items.append((gi, p, sbi, p == p0, (p == p1 - 1) and not last))
                        if last:
                            items.append((gi, "S", sbi, False, True))
                        p0 = p1
                assert max(x[2] for x in sbs) <= QB

                def load_kv(gi):
                    i = gi % 2
                    if isA:
                        pg.dma("sp", kq[i][:], kT[gi * 128:(gi + 1) * 128, :], dst=b_kq[i])
                        pg.dma("sp", vv[i][:, 0:MT - 1, :], v_a[gi][:, 0:MT - 1, :], dst=b_vv[i])
                        pg.dma("sp", vv[i][0:NS, MT - 1, :], v_a[gi][0:NS, MT - 1, :], dst=b_vv[i])
                    else:
                        pg.dma("sp", kq[i][0:64, :], kT[gi * 64:(gi + 1) * 64, :], dst=b_kq[i])
                        pg.dma("sp", vv[i][:, 0:MT - 1, :], v_b[gi][:, 0:MT - 1, :], dst=b_vv[i])
                        pg.dma("sp", vv[i][0:NS, MT - 1, :], v_b[gi][0:NS, MT - 1, :], dst=b_vv[i])

                def load_q(sbi):
                    if sbi >= len(sbs):
                        return
                    gi, tok0, ntok = sbs[sbi]
                    j = sbi % 2
                    if isA:
                        pg.dma("sp", qq[j][:, 0, 0:ntok], qT[gi * 128:(gi + 1) * 128, tok0:tok0 + ntok], dst=b_qq[j])
                    else:
                        pg.dma("sp", qq[j][0:64, :, 0:ntok],
                               qT[gi * 256:(gi + 1) * 256, tok0:tok0 + ntok].rearrange("(g d) t -> d g t", g=4),
                               dst=b_qq[j])

                def store_o(sbi):
                    gi, tok0, ntok = sbs[sbi]
                    j = sbi % 2
                    if isA:
                        pg.dma("pool", oT[gi * 128:(gi + 1) * 128, tok0:tok0 + ntok], ost[j][:, 0, 0:ntok],
                               src=b_ost[j])
                    else:
                        pg.dma("pool",
                               oT[gi * 256:(gi + 1) * 256, tok0:tok0 + ntok].rearrange("(g d) t -> d g t", g=4),
                               ost[j][0:64, :, 0:ntok], src=b_ost[j])

                def slots_of(it):
                    gi, p = it[0], it[1]
                    i = gi % 2
                    out = []
                    if p == "S":
                        for s in range(NSL - 1):
                            out.append((s, 128, kcT[0:DH, gi, s * 128:(s + 1) * 128],
                                        vc[:, s, gi * DH:(gi + 1) * DH]))
                        out.append((NSL - 1, NS, kq[i][0:DH, NTP:NTP + NS], vv[i][0:NS, MT - 1, :]))
                    else:
                        for s in range(NSL):
                            m = p - (NSL - 1) + s
                            if m < 0:
                                continue
                            out.append((s, 128, kq[i][0:DH, m * 128:(m + 1) * 128], vv[i][:, m, :]))
                    return out

                def stage1(idx):
                    it = items[idx]
                    gi, p, sbi, first, lastf = it
                    i = gi % 2
                    j = sbi % 2
                    bi = idx % 2
                    sl = slots_of(it)
                    Q = NS if p == "S" else 128
                    q0 = (NTP if p == "S" else p * 128) - sbs[sbi][1]
                    if first:
                        load_q(sbi + 1)
                    dl = []

                    def f(e):
                        for (s, nk, kap, vap) in sl:
                            for g in range(G):
                                c = s * SW + g * 128
                                r = e.matmul(sps[bi][0:nk, c:c + Q], kap, qq[j][0:DH, g, q0:q0 + Q],
                                             start=True, stop=True)
                        return r
                    pg.op("pe", f, reads=[b_kq[i], b_qq[j], b_kcT], writes=[b_sps[bi]])
                    if p == "S":
                        for (s, nk, kap, vap) in sl:
                            def v3(t, s=s, nk=nk):
                                return t[0:nk, s * SW:(s + 1) * SW].rearrange("p (g q) -> p g q", g=G)[:, :, 0:NS]
                            sg = (s * SW) // 512
                            pg.op("act", lambda e, v3=v3: e.activation(out=v3(pf[bi]), in_=v3(sps[bi]), func=AF.Exp,
                                                                       scale=scale),
                                  reads=[b_sps[bi]], writes=[b_pf[bi][sg]])
                            dl.append((lambda e, v3=v3: e.tensor_tensor(out=v3(pb[bi]), in0=v3(pf[bi]),
                                                                        in1=v3(EB[:, gi, :]), op=ALU.mult),
                                       [b_pf[bi][sg], b_EB], [b_pb[bi][sg]], "dve"))
                    else:
                        c0 = sl[0][0] * SW
                        a = c0
                        while a < NCOL:
                            bnd = min(NCOL, (a // 512 + 1) * 512)
                            pg.op("act", lambda e, a=a, bnd=bnd: e.activation(out=pf[bi][:, a:bnd],
                                                                              in_=sps[bi][:, a:bnd], func=AF.Exp,
                                                                              scale=scale),
                                  reads=[b_sps[bi]], writes=[b_pf[bi][a // 512]])
                            dl.append((lambda e, a=a, bnd=bnd: e.tensor_tensor(out=pb[bi][:, a:bnd],
                                                                               in0=pf[bi][:, a:bnd],
                                                                               in1=EB[:, gi, a:bnd], op=ALU.mult),
                                       [b_pf[bi][a // 512], b_EB], [b_pb[bi][a // 512]],
                                       ("pool" if (a >= 512 and not isA) else "dve")))
                            a = bnd
                    return dl

                def stage2(idx):
                    it = items[idx]
                    gi, p, sbi, first, lastf = it
                    i = gi % 2
                    j = sbi % 2
                    bi = idx % 2
                    sl = slots_of(it)
                    Q = NS if p == "S" else 128
                    q0 = (NTP if p == "S" else p * 128) - sbs[sbi][1]

                    def f(e):
                        for g in range(G if Q != 128 else 1):
                            for k, (s, nk, kap, vap) in enumerate(sl):
                                if Q == 128:
                                    rhs = pb[bi][0:nk, s * SW:(s + 1) * SW]
                                    o1 = acc[bi][0:DH, 0:SW]
                                    o2 = den[bi][0:DH, 0:SW]
                                else:
                                    c = s * SW + g * 128
                                    rhs = pb[bi][0:nk, c:c + Q]
                                    o1 = acc[bi][0:DH, g * 128:g * 128 + Q]
                                    o2 = den[bi][0:DH, g * 128:g * 128 + Q]
                                e.matmul(o1, vap, rhs, start=(k == 0), stop=(k == len(sl) - 1))
                                r = e.matmul(o2, ones_b[0:nk, 0:DH], rhs, start=(k == 0),
                                             stop=(k == len(sl) - 1) and isA)
                            if not isA:
                                if Q == 128:
                                    o2 = den[bi][0:DH, 0:SW]
                                    eh, el = es_hi[0:1, gi, :], es_lo[0:1, gi, :]
                                else:
                                    o2 = den[bi][0:DH, g * 128:g * 128 + Q]
                                    eh, el = es_hi[0:1, gi, g * 128:g * 128 + Q], es_lo[0:1, gi, g * 128:g * 128 + Q]
                                e.matmul(o2, ones_b[0:1, 0:DH], eh, start=False, stop=False)
                                r = e.matmul(o2, ones_b[0:1, 0:DH], el, start=False, stop=True)
                        return r
                    pg.op("pe", f, reads=[b_pb[bi][0], b_pb[bi][1], b_vv[i], b_vc, c_ones] + ([] if isA else [b_eshl]),
                          writes=[b_acc[bi], b_den[bi]])

                    def vw(t):
                        return t[0:DH, 0:SW].rearrange("p (g q) -> p g q", g=G)[:, :, 0:Q]
                    dl = []
                    if isA:
                        dl.append((lambda e: e.reciprocal(out=vw(rd[bi]), in_=vw(den[bi])),
                                   [b_den[bi]], [b_rd[bi]], "dve"))
                    else:
                        dl.append((lambda e: e.reciprocal(out=vw(rd[bi]), in_=vw(den[bi])),
                                   [b_den[bi]], [b_rd[bi]], "dve"))
                    dl.append((lambda e: e.tensor_tensor(out=ost[j][0:DH, :, q0:q0 + Q], in0=vw(acc[bi]),
                                                         in1=vw(rd[bi]), op=ALU.mult),
                               [b_acc[bi], b_rd[bi]], [b_ost[j]], "dve"))
                    return dl

                def emit_dve(lst):
                    for (fn, rd_, wr_, en_) in lst:
                        pg.op(en_, fn, reads=rd_, writes=wr_)

                load_kv(0)
                load_q(0)
                emit_dve(stage1(0))
                prev_d2, prev_store = [], None
                for idx in range(len(items)):
                    gi, p = items[idx][0], items[idx][1]
                    if p == 0 and gi + 1 < NG:
                        load_kv(gi + 1)
                    d1 = stage1(idx + 1) if idx + 1 < len(items) else []
                    d2 = stage2(idx)
                    mix = []
                    while d1 or prev_d2:
                        if prev_d2:
                            mix.append(prev_d2.pop(0))
                        if d1:
                            mix.append(d1.pop(0))
                    emit_dve(mix)
                    if prev_store is not None:
                        store_o(prev_store)
                        prev_store = None
                    prev_d2 = d2
                    if items[idx][4]:
                        prev_store = items[idx][2]
                emit_dve(prev_d2)
                if prev_store is not None:
                    store_o(prev_store)
                pg.end_phase([miscsem, gsem])

            with contextlib.ExitStack() as ph:
                x1 = ph.enter_context(sbt("x1", [128, KC, 512], F32))
                aT = [ph.enter_context(sbt("aT%d" % i, [128, KC, 512], BF16)) for i in range(1)]
                uT = ph.enter_context(sbt("uT", [128, 64, 512], BF16))
                wt = [ph.enter_context(sbt("wtc%d" % i, [128, KC * 512], BF16)) for i in range(3)]
                sq = [ph.enter_context(sbt("sqc%d" % i, [128, 512], BF16)) for i in range(4)]
                rs = ph.enter_context(sbt("rsc", [128, 512], F32))
                x0c = [ph.enter_context(sbt("x0c%d" % i, [128, 512], F32)) for i in range(3)]
                rl = [ph.enter_context(sbt("rl%d" % i, [128, 512], F32)) for i in range(3)]
                xo = [ph.enter_context(sbt("xo%d" % i, [128, 512], F32)) for i in range(3)]
                ps = ph.enter_context(pst("psC", [128, 4096], F32))
                b_x1, b_aT, b_uT, b_wt = pg.buf("x1"), pg.bufs("aT", 1), pg.buf("uT"), pg.bufs("wtc", 3)
                b_sq, b_rs, b_x0c, b_rl, b_xo = pg.bufs("sq", 4), pg.buf("rs"), pg.bufs("x0c", 3), pg.bufs("rl", 3), pg.bufs("xo", 3)
                b_ps = pg.bufs("ps", 8)
                wblocks = []
                v16 = lambda t: t[:]
                for _blk in blocks:
                    for b in range(4):
                        wblocks.append((wo_s[par][b].rearrange("p k n -> p (k n)"), v16, castsems[("o", par)]))
                    for b in range(16):
                        wblocks.append((wu_s[par][b].rearrange("p k n -> p (k n)"), v16, castsems[("u", par)]))
                    for b in range(16):
                        wblocks.append((wd_s[par][b].rearrange("p k n -> p (k n)"), v16, castsems[("d", par)]))
                flush_casts(layer, "qoud")
                pg.hook_every = 3
                ws = WStream(pg, wt, b_wt, wblocks, None)
                cnt = dict(ps=1, x0=0, rl=0, xo=0)

                def nextps():
                    bi = 1 + (cnt["ps"] - 1) % 7
                    cnt["ps"] += 1
                    return bi

                for bidx, (t0, n) in enumerate(blocks):
                    ai = 0
                    pg.dma("sp", aT[ai][:, :, 0:n], oT[:, t0:t0 + n].rearrange("(c p) t -> p c t", p=128),
                           dst=b_aT[ai])
                    sqpend = []

                    def ones_mm(mm, n_):
                        pg.op("pe", lambda e, mm=mm, n_=n_: e.matmul(ps[:, 0:n_], ones_m[:], sq[mm % 4][:, 0:n_],
                                                                     start=(mm == 0), stop=(mm == KC - 1)),
                              reads=[b_sq[mm % 4], c_ones], writes=[b_ps[0]])

                    for b in range(4):
                        W, bW = ws.get()
                        Wv = W[:].rearrange("p (k n) -> p k n", k=KC)
                        for j in range(4):
                            m = 4 * b + j
                            bi = nextps()

                            def f(e, j=j, bi=bi, Wv=Wv, ai=ai, n=n):
                                for kc in range(KC):
                                    r = e.matmul(ps[:, bi * 512: bi * 512 + n], Wv[:, kc, j * 128:(j + 1) * 128],
                                                 aT[ai][:, kc, 0:n], start=(kc == 0), stop=(kc == KC - 1))
                                return r
                            pg.op("pe", f, reads=[bW, b_aT[ai]], writes=[b_ps[bi]])
                            xi = cnt["x0"] % 3
                            cnt["x0"] += 1
                            pg.dma("sp", x0c[xi][:, 0:n], xs[:, m, t0:t0 + n], dst=b_x0c[xi])
                            pg.op("dve", lambda e, m=m, bi=bi, xi=xi, n=n: e.tensor_tensor(
                                out=x1[:, m, 0:n], in0=ps[:, bi * 512: bi * 512 + n], in1=x0c[xi][:, 0:n], op=ALU.add),
                                reads=[b_ps[bi], b_x0c[xi]], writes=[b_x1])
                            pg.op("act", lambda e, m=m, n=n: e.activation(out=sq[m % 4][:, 0:n], in_=x1[:, m, 0:n],
                                                                          func=AF.Square),
                                  reads=[b_x1], writes=[b_sq[m % 4]])
                            sqpend.append(m)
                            if len(sqpend) > 2:
                                ones_mm(sqpend.pop(0), n)
                        ws.prefetch()
                    while sqpend:
                        ones_mm(sqpend.pop(0), n)
                    pg.op("dve", lambda e, n=n: e.tensor_scalar(out=rs[:, 0:n], in0=ps[:, 0:n], scalar1=EPS, scalar2=None,
                                                                op0=ALU.add), reads=[b_ps[0]], writes=[b_rs])
                    pg.op("act", lambda e, n=n: e.sqrt(out=rs[:, 0:n], in_=rs[:, 0:n]), reads=[b_rs], writes=[b_rs])
                    pg.op("dve", lambda e, n=n: e.reciprocal(out=rs[:, 0:n], in_=rs[:, 0:n]), reads=[b_rs], writes=[b_rs])
                    for c in range(KC):
                        pg.op("dve", lambda e, c=c, n=n, ai=ai: e.scalar_tensor_tensor(
                            out=aT[ai][:, c, 0:n], in0=x1[:, c, 0:n], scalar=g_ffn[:, c:c + 1], in1=rs[:, 0:n],
                            op0=ALU.mult, op1=ALU.mult), reads=[b_x1, b_rs, c_gn], writes=[b_aT[ai]])
                    for b in range(16):
                        W, bW = ws.get()
                        Wv = W[:].rearrange("p (k n) -> p k n", k=KC)
                        for j in range(4):
                            fc = 4 * b + j
                            bi = nextps()

                            def f(e, j=j, bi=bi, Wv=Wv, ai=ai, n=n):
                                for kc in range(KC):
                                    r = e.matmul(ps[:, bi * 512: bi * 512 + n], Wv[:, kc, j * 128:(j + 1) * 128],
                                                 aT[ai][:, kc, 0:n], start=(kc == 0), stop=(kc == KC - 1))
                                return r
                            pg.op("pe", f, reads=[bW, b_aT[ai]], writes=[b_ps[bi]])
                            ri = cnt["rl"] % 3
                            cnt["rl"] += 1
                            pg.op("act", lambda e, ri=ri, bi=bi, n=n: e.activation(
                                out=rl[ri][:, 0:n], in_=ps[:, bi * 512: bi * 512 + n], func=AF.Relu),
                                reads=[b_ps[bi]], writes=[b_rl[ri]])
                            eng = "pool" if (fc % 2 == 0) else "dve"
                            pg.op(eng, lambda e, ri=ri, fc=fc, n=n: e.tensor_tensor(
                                out=uT[:, fc, 0:n], in0=rl[ri][:, 0:n], in1=rl[ri][:, 0:n], op=ALU.mult),
                                reads=[b_rl[ri]], writes=[b_uT])
                        ws.prefetch()
                    for m in range(16):
                        W, bW = ws.get()
                        Wv = W[:].rearrange("p (k n) -> p k n", k=64)
                        bi = nextps()

                        def f(e, bi=bi, Wv=Wv, n=n):
                            for fc in range(64):
                                r = e.matmul(ps[:, bi * 512: bi * 512 + n], Wv[:, fc, :], uT[:, fc, 0:n],
                                             start=(fc == 0), stop=(fc == 63))
                            return r
                        pg.op("pe", f, reads=[bW, b_uT], writes=[b_ps[bi]])
                        oi = cnt["xo"] % 3
                        cnt["xo"] += 1
                        pg.op("dve", lambda e, m=m, bi=bi, oi=oi, n=n: e.tensor_tensor(
                            out=xo[oi][:, 0:n], in0=ps[:, bi * 512: bi * 512 + n], in1=x1[:, m, 0:n], op=ALU.add),
                            reads=[b_ps[bi], b_x1], writes=[b_xo[oi]])
                        pg.dma("pool", xs[:, m, t0:t0 + n], xo[oi][:, 0:n], src=b_xo[oi])
                        ws.prefetch()
                pg.end_phase([miscsem, gsem])

        with contextlib.ExitStack() as ph:
            xf = [ph.enter_context(sbt("xf%d" % i, [128, KC, 128], F32)) for i in range(2)]
            yt = [ph.enter_context(sbt("yt%d" % i, [128, D], F32)) for i in range(2)]
            junk = ph.enter_context(sbt("junk", [128, D], F32))
            gfull = ph.enter_context(sbt("gfull", [128, D], F32))
            ssq = [ph.enter_context(sbt("ssq%d" % i, [128, 8], F32)) for i in range(2)]
            ps = ph.enter_context(pst("psE", [128, 4096], F32))
            b_xf, b_yt, b_junk, b_g, b_ssq = pg.bufs("xf", 2), pg.bufs("yt", 2), pg.buf("junk"), pg.buf("gfull"), pg.bufs("ssq", 2)
            b_ps = pg.bufs("psh", 2)
            pg.dma("sp", gfull[:], bass.AP(gfin, 0, [[0, 128], [1, D]]), dst=b_g)
            for tt in range(MT):
                rows = 128 if tt < MT - 1 else NS
                i = tt % 2
                pg.dma("sp", xf[i][:, :, 0:rows], xs[:, :, tt * 128: tt * 128 + rows], dst=b_xf[i])
                psv = ps[:, i * 2048:(i + 1) * 2048]

                def f(e, i=i, rows=rows, psv=psv):
                    for c in range(KC):
                        r = e.transpose(psv[0:rows, c * 128:(c + 1) * 128], xf[i][:, c, 0:rows], ident[:])
                    return r
                pg.op("pe", f, reads=[b_xf[i], c_ident], writes=[b_ps[i]])

                pg.op("act", lambda e, i=i: e.memzero(ssq[i][:]), writes=[b_ssq[i]])
                for k in range(4):
                    pg.op("act", lambda e, i=i, k=k, rows=rows, psv=psv: e.activation(
                        out=junk[0:rows, k * 512:(k + 1) * 512], in_=psv[0:rows, k * 512:(k + 1) * 512],
                        func=AF.Square, accum_out=ssq[i][0:rows, k:k + 1]),
                        reads=[b_ps[i], b_ssq[i]], writes=[b_junk, b_ssq[i]])
                pg.op("dve", lambda e, i=i, rows=rows: e.tensor_reduce(
                    out=ssq[i][0:rows, 4:5], in_=ssq[i][0:rows, 0:4], axis=mybir.AxisListType.X, op=ALU.add),
                    reads=[b_ssq[i]], writes=[b_ssq[i]])
                pg.op("dve", lambda e, i=i, rows=rows: e.tensor_scalar(
                    out=ssq[i][0:rows, 5:6], in0=ssq[i][0:rows, 4:5], scalar1=1.0 / D, scalar2=EPS,
                    op0=ALU.mult, op1=ALU.add), reads=[b_ssq[i]], writes=[b_ssq[i]])
                pg.op("act", lambda e, i=i, rows=rows: e.sqrt(out=ssq[i][0:rows, 6:7], in_=ssq[i][0:rows, 5:6]),
                      reads=[b_ssq[i]], writes=[b_ssq[i]])
                pg.op("dve", lambda e, i=i, rows=rows: e.reciprocal(out=ssq[i][0:rows, 7:8], in_=ssq[i][0:rows, 6:7]),
                      reads=[b_ssq[i]], writes=[b_ssq[i]])

                def fy(e, i=i, rows=rows, psv=psv):
                    for k in range(4):
                        r = e.scalar_tensor_tensor(out=yt[i][0:rows, k * 512:(k + 1) * 512],
                                                   in0=psv[0:rows, k * 512:(k + 1) * 512],
                                                   scalar=ssq[i][0:rows, 7:8], in1=gfull[0:rows, k * 512:(k + 1) * 512],
                                                   op0=ALU.mult, op1=ALU.mult)
                    return r
                pg.op("dve", fy, reads=[b_ps[i], b_ssq[i], b_g], writes=[b_yt[i]])
                dst = y_p[tt * 128:(tt + 1) * 128, :] if tt < MT - 1 else y_s
                pg.dma("pool", dst, yt[i][0:rows, :], src=b_yt[i])
            pg.end_phase([miscsem, gsem])
    return nc


def kernel(x_prompt, x_sample, cache_a_k, cache_a_v, cache_b_k, cache_b_v,
           norm_mix, norm_ffn, norm_final, a_w_qkv, a_w_o, a_rel_bias,
           b_w_qkv, b_w_o, b_sinks, w_up, w_down, _cfg=None):
    cfg = dict(CFG)
    if _cfg:
        cfg.update(_cfg)
    DEPTH, NTP, SEQ, SPLIT = cfg["DEPTH"], cfg["NTP"], cfg["SEQ"], cfg["SPLIT"]
    NA = (DEPTH + 1) // 2
    NB = DEPTH // 2
    f32 = np.float32
    A = lambda a: np.ascontiguousarray(np.asarray(a, dtype=f32))
    x_prompt, x_sample = A(x_prompt), A(x_sample)
    BATCH = x_prompt.shape[0]
    ncores = 8
    gains = np.concatenate([A(norm_mix)[:DEPTH].reshape(DEPTH, KC, 128), A(norm_ffn)[:DEPTH].reshape(DEPTH, KC, 128)],
                           axis=0)
    gains = np.ascontiguousarray(gains.transpose(2, 0, 1).reshape(128, 2 * DEPTH * KC))
    nidx = np.clip(127 - np.arange(768), -128, 128) + 128
    relext = np.ascontiguousarray(A(a_rel_bias)[:NA][:, :, nidx].reshape(NA, A_H * 768))
    nbm = max(NB, 1)
    common = dict(
        gains=gains, gfin=A(norm_final), relext=relext, sinks=A(b_sinks)[:nbm],
        w_aqkv=A(a_w_qkv)[:NA], w_ao=A(a_w_o)[:NA], w_bqkv=A(b_w_qkv)[:nbm], w_bo=A(b_w_o)[:nbm],
        w_up=A(w_up)[:DEPTH], w_dn=A(w_down)[:DEPTH])
    cache_a_k, cache_a_v, cache_b_k, cache_b_v = A(cache_a_k), A(cache_a_v), A(cache_b_k), A(cache_b_v)
    if NB == 0:
        cache_b_k = cache_b_v = np.zeros((1, ncores, B_CL, B_HKV, B_DH), f32)
        common.update(sinks=np.zeros((1, B_HQ), f32), w_bqkv=np.zeros((1, D, 3072), f32), w_bo=np.zeros((1, D, D), f32))
    in_maps = []
    for i in range(ncores):
        if SPLIT:
            b, half = i // 2, i % 2
            st = 0 if half == 0 else SEQ - NTP
        else:
            b, st = i, 0
        m = dict(common)
        m["x_p"] = np.ascontiguousarray(x_prompt[b, st:st + NTP])
        m["x_s"] = np.ascontiguousarray(x_sample[i])
        m["ca_k"] = np.ascontiguousarray(cache_a_k[:NA, i].reshape(NA, A_CL, D))
        m["ca_v"] = np.ascontiguousarray(cache_a_v[:NA, i].reshape(NA, A_CL, D))
        m["cb_k"] = np.ascontiguousarray(cache_b_k[:nbm, i].reshape(nbm, B_CL, 512))
        m["cb_v"] = np.ascontiguousarray(cache_b_v[:nbm, i].reshape(nbm, B_CL, 512))
        in_maps.append(m)
    nc = build_program(cfg)
    res = run_bass_kernel_spmd(nc, in_maps, core_ids=list(range(ncores)))
    R = res.results
    y_prompt = np.empty((BATCH, SEQ, D), f32)
    sakp = np.empty((NA, BATCH, A_CL, A_H, A_DH), f32)
    savp = np.empty_like(sakp)
    sbkp = np.empty((NB, BATCH, B_CL, B_HKV, B_DH), f32)
    sbvp = np.empty_like(sbkp)
    for i in range(ncores):
        if SPLIT:
            b, half = i // 2, i % 2
            if half == 0:
                y_prompt[b, 0:NTP] = R[i]["y_p"]
                continue
            y_prompt[b, NTP:SEQ] = R[i]["y_p"][2 * NTP - SEQ:]
        else:
            b = i
            y_prompt[b] = R[i]["y_p"]
        sakp[:, b] = R[i]["sa_kp"].reshape(NA, A_CL, A_H, A_DH)
        savp[:, b] = R[i]["sa_vp"].reshape(NA, A_CL, A_H, A_DH)
        if NB:
            sbkp[:, b] = R[i]["sb_kp"][:NB].reshape(NB, B_CL, B_HKV, B_DH)
            sbvp[:, b] = R[i]["sb_vp"][:NB].reshape(NB, B_CL, B_HKV, B_DH)
    y_sample = np.stack([R[i]["y_s"] for i in range(ncores)]).astype(f32)
    saks = np.stack([R[i]["sa_ks"].reshape(NA, A_CL, A_H, A_DH) for i in range(ncores)], axis=1)
    savs = np.stack([R[i]["sa_vs"].reshape(NA, A_CL, A_H, A_DH) for i in range(ncores)], axis=1)
    sbks = np.stack([R[i]["sb_ks"][:NB].reshape(NB, B_CL, B_HKV, B_DH) for i in range(ncores)], axis=1)
    sbvs = np.stack([R[i]["sb_vs"][:NB].reshape(NB, B_CL, B_HKV, B_DH) for i in range(ncores)], axis=1)
    return (y_prompt, y_sample, sakp, savp, sbkp, sbvp,
            np.ascontiguousarray(saks), np.ascontiguousarray(savs),
            np.ascontiguousarray(sbks), np.ascontiguousarray(sbvs))
```
